# Optimizing a Trainium2 kernel written in Bass

```python
import jax
import jax.numpy as jnp
from jax import lax
import numpy as np

D_MODEL = 2048
BATCH = 2
SEQ = 16384
DEPTH = 4

GRID_W = 64
CTX_LEN = 256
ATTN_WIDTH = D_MODEL // 2
FOURIER_WIDTH = D_MODEL - ATTN_WIDTH
V_HEAD_DIM = 128
N_HEADS = ATTN_WIDTH // V_HEAD_DIM
QK_NOPE_DIM = 128
QK_ROPE_DIM = 64
QK_DIM = QK_NOPE_DIM + QK_ROPE_DIM
Q_LORA_RANK = 512
KV_LORA_RANK = 256
FOURIER_GROUPS = 4
FOURIER_GROUP_CH = FOURIER_WIDTH // FOURIER_GROUPS
IDX_Q = Q_LORA_RANK
IDX_R = IDX_Q + KV_LORA_RANK + QK_ROPE_DIM
IN_COLS = IDX_R + FOURIER_WIDTH
D_FF = 5632
N_MOD = 9
AXIS_FREQS = QK_ROPE_DIM // 4
ROPE_THETA = 10000.0
Q_BLOCK = 128
SM_SCALE = QK_DIM ** -0.5
RMS_EPS = 1e-6

kernel_name = "hymba_mla_fnet_macaron_dit"


def rms_norm(x, g, eps=RMS_EPS):
    xf = x.astype(jnp.float32)
    y = xf * lax.rsqrt(jnp.mean(xf * xf, axis=-1, keepdims=True) + eps)
    return (y * g.astype(jnp.float32)).astype(x.dtype)


def adaln(cond, w, b):
    m = jax.nn.silu(cond) @ w + b
    return jnp.split(m[..., None, :], N_MOD, axis=-1)


def modulate(h, shift, scale):
    return h * (1.0 + scale) + shift


def swiglu(h, w_gate, w_up, w_down):
    return (jax.nn.silu(h @ w_gate) * (h @ w_up)) @ w_down


def axial_rope_tables(n_tokens, dtype):
    rows = n_tokens // GRID_W
    row = jnp.broadcast_to(jnp.arange(rows, dtype=jnp.float32)[:, None], (rows, GRID_W)).reshape(-1)
    col = jnp.broadcast_to(jnp.arange(GRID_W, dtype=jnp.float32)[None, :], (rows, GRID_W)).reshape(-1)
    inv_freq = ROPE_THETA ** (-jnp.arange(AXIS_FREQS, dtype=jnp.float32) / AXIS_FREQS)
    ang_r = row[:, None] * inv_freq
    ang_c = col[:, None] * inv_freq
    return tuple(t.astype(dtype) for t in (jnp.cos(ang_r), jnp.sin(ang_r), jnp.cos(ang_c), jnp.sin(ang_c)))


def rotate_half_split(x, cos, sin):
    x1, x2 = jnp.split(x, 2, axis=-1)
    cos = cos[None, :, None, :]
    sin = sin[None, :, None, :]
    return jnp.concatenate([x1 * cos - x2 * sin, x1 * sin + x2 * cos], axis=-1)


def apply_axial_rope(x, rope):
    cos_r, sin_r, cos_c, sin_c = rope
    xr, xc = jnp.split(x, 2, axis=-1)
    return jnp.concatenate([rotate_half_split(xr, cos_r, sin_r), rotate_half_split(xc, cos_c, sin_c)], axis=-1)


def with_rope(t, rope):
    return jnp.concatenate([t[..., :QK_NOPE_DIM], apply_axial_rope(t[..., QK_NOPE_DIM:], rope)], axis=-1)


def mla_queries(q_lat, g_lat, w_uq, g_qn, rope):
    B, L, _ = q_lat.shape
    q = (rms_norm(q_lat, g_lat) @ w_uq).reshape(B, L, N_HEADS, QK_DIM)
    q = rms_norm(q, g_qn)
    return q if rope is None else with_rope(q, rope)


def mla_keys_values(kv_part, g_lat, w_ukv, g_kn, rope):
    B, L, _ = kv_part.shape
    kv_lat, k_rope = kv_part[..., :KV_LORA_RANK], kv_part[..., KV_LORA_RANK:]
    kv = (rms_norm(kv_lat, g_lat) @ w_ukv).reshape(B, L, N_HEADS, QK_NOPE_DIM + V_HEAD_DIM)
    k_nope, v = kv[..., :QK_NOPE_DIM], kv[..., QK_NOPE_DIM:]
    k_rope = jnp.broadcast_to(k_rope[:, :, None, :], (B, L, N_HEADS, QK_ROPE_DIM))
    k = rms_norm(jnp.concatenate([k_nope, k_rope], axis=-1), g_kn)
    return (k if rope is None else with_rope(k, rope)), v


def attend(q, k, v):
    s = jnp.einsum('bqhd,bkhd->bhqk', q, k, preferred_element_type=jnp.float32) * SM_SCALE
    p = jax.nn.softmax(s, axis=-1).astype(v.dtype)
    return jnp.einsum('bhqk,bkhd->bqhd', p, v)


def latent_attention(q, k, v):
    B, S, H, Dq = q.shape
    n_blk = S // Q_BLOCK
    qb = q.reshape(B, n_blk, Q_BLOCK, H, Dq).transpose(1, 0, 2, 3, 4)
    o = lax.map(lambda qi: attend(qi, k, v), qb)
    return o.transpose(1, 0, 2, 3, 4).reshape(B, S, H * V_HEAD_DIM)


def fourier_mix(f, w_f):
    B, L, _ = f.shape
    fg = f.reshape(B, L, FOURIER_GROUPS, FOURIER_GROUP_CH).transpose(0, 2, 1, 3).astype(jnp.float32)
    mixed = jnp.fft.fft2(fg, norm="ortho").real.astype(f.dtype)
    out = jnp.einsum('bglc,gcd->blgd', mixed, w_f)
    return out.reshape(B, L, FOURIER_WIDTH)


def setup_inputs(seed: int = 0) -> dict:
    key = jax.random.key(seed)
    ks = jax.random.split(key, 24)
    f32 = jnp.float32

    def nrm(k, shape, scale):
        return jax.random.normal(k, shape, f32) * scale

    def gain(k, shape):
        return 1.0 + 0.02 * jax.random.normal(k, shape, f32)

    return {
        "x": nrm(ks[0], (BATCH, SEQ, D_MODEL), 1.0),
        "c": nrm(ks[1], (BATCH, D_MODEL), 1.0),
        "ctx": nrm(ks[2], (BATCH, CTX_LEN, D_MODEL), 1.0),
        "c_ctx": nrm(ks[3], (D_MODEL,), 1.0),
        "w_ada": nrm(ks[4], (DEPTH, D_MODEL, N_MOD * D_MODEL), 0.5 * D_MODEL ** -0.5),
        "b_ada": nrm(ks[5], (DEPTH, N_MOD * D_MODEL), 0.02),
        "norm_g": gain(ks[6], (DEPTH, 3, D_MODEL)),
        "ffn1_w_gate": nrm(ks[7], (DEPTH, D_MODEL, D_FF), D_MODEL ** -0.5),
        "ffn1_w_up": nrm(ks[8], (DEPTH, D_MODEL, D_FF), D_MODEL ** -0.5),
        "ffn1_w_down": nrm(ks[9], (DEPTH, D_FF, D_MODEL), D_FF ** -0.5),
        "ffn2_w_gate": nrm(ks[10], (DEPTH, D_MODEL, D_FF), D_MODEL ** -0.5),
        "ffn2_w_up": nrm(ks[11], (DEPTH, D_MODEL, D_FF), D_MODEL ** -0.5),
        "ffn2_w_down": nrm(ks[12], (DEPTH, D_FF, D_MODEL), D_FF ** -0.5),
        "w_in": nrm(ks[13], (DEPTH, D_MODEL, IN_COLS), D_MODEL ** -0.5),
        "q_lat_g": gain(ks[14], (DEPTH, Q_LORA_RANK)),
        "w_uq": nrm(ks[15], (DEPTH, Q_LORA_RANK, N_HEADS * QK_DIM), Q_LORA_RANK ** -0.5),
        "kv_lat_g": gain(ks[16], (DEPTH, KV_LORA_RANK)),
        "w_ukv": nrm(ks[17], (DEPTH, KV_LORA_RANK, N_HEADS * (QK_NOPE_DIM + V_HEAD_DIM)), KV_LORA_RANK ** -0.5),
        "q_norm_g": gain(ks[18], (DEPTH, QK_DIM)),
        "k_norm_g": gain(ks[19], (DEPTH, QK_DIM)),
        "w_fourier": nrm(ks[20], (DEPTH, FOURIER_GROUPS, FOURIER_GROUP_CH, FOURIER_GROUP_CH), FOURIER_GROUP_CH ** -0.5),
        "w_o": nrm(ks[21], (DEPTH, ATTN_WIDTH + FOURIER_WIDTH, D_MODEL), (ATTN_WIDTH + FOURIER_WIDTH) ** -0.5),
    }


def reference(x, c, ctx, c_ctx, w_ada, b_ada, norm_g,
              ffn1_w_gate, ffn1_w_up, ffn1_w_down, ffn2_w_gate, ffn2_w_up, ffn2_w_down,
              w_in, q_lat_g, w_uq, kv_lat_g, w_ukv, q_norm_g, k_norm_g, w_fourier, w_o):
    rope = axial_rope_tables(x.shape[1], x.dtype)
    xc = ctx
    for l in range(DEPTH):
        last = l == DEPTH - 1
        mod = adaln(c, w_ada[l], b_ada[l])
        modc = adaln(c_ctx, w_ada[l], b_ada[l])
        ffn1 = (ffn1_w_gate[l], ffn1_w_up[l], ffn1_w_down[l])
        ffn2 = (ffn2_w_gate[l], ffn2_w_up[l], ffn2_w_down[l])

        x = x + 0.5 * mod[2] * swiglu(modulate(rms_norm(x, norm_g[l, 0]), mod[0], mod[1]), *ffn1)
        xc = xc + 0.5 * modc[2] * swiglu(modulate(rms_norm(xc, norm_g[l, 0]), modc[0], modc[1]), *ffn1)

        h = modulate(rms_norm(x, norm_g[l, 1]), mod[3], mod[4])
        hc = modulate(rms_norm(xc, norm_g[l, 1]), modc[3], modc[4])
        proj = h @ w_in[l]
        q = mla_queries(proj[..., :IDX_Q], q_lat_g[l], w_uq[l], q_norm_g[l], rope)
        k, v = mla_keys_values(proj[..., IDX_Q:IDX_R], kv_lat_g[l], w_ukv[l], k_norm_g[l], rope)
        if last:
            kc, vc = mla_keys_values(hc @ w_in[l][:, IDX_Q:IDX_R], kv_lat_g[l], w_ukv[l], k_norm_g[l], None)
        else:
            proj_c = hc @ w_in[l]
            kc, vc = mla_keys_values(proj_c[..., IDX_Q:IDX_R], kv_lat_g[l], w_ukv[l], k_norm_g[l], None)

        attn = latent_attention(q, jnp.concatenate([k, kc], axis=1), jnp.concatenate([v, vc], axis=1))
        four = fourier_mix(proj[..., IDX_R:], w_fourier[l])
        x = x + mod[5] * (jnp.concatenate([attn, four], axis=-1) @ w_o[l])

        x = x + 0.5 * mod[8] * swiglu(modulate(rms_norm(x, norm_g[l, 2]), mod[6], mod[7]), *ffn2)

        if not last:
            qc = mla_queries(proj_c[..., :IDX_Q], q_lat_g[l], w_uq[l], q_norm_g[l], None)
            Bc, Lc = qc.shape[0], qc.shape[1]
            attn_c = attend(qc, kc, vc).reshape(Bc, Lc, ATTN_WIDTH)
            four_c = fourier_mix(proj_c[..., IDX_R:], w_fourier[l])
            xc = xc + modc[5] * (jnp.concatenate([attn_c, four_c], axis=-1) @ w_o[l])
            xc = xc + 0.5 * modc[8] * swiglu(modulate(rms_norm(xc, norm_g[l, 2]), modc[6], modc[7]), *ffn2)
    return x
```

```python
from contextlib import ExitStack
import numpy as np
import concourse.bass as bass
import concourse.mybir as mybir
from concourse.bass_utils import run_bass_kernel_spmd

F32 = mybir.dt.float32
BF16 = mybir.dt.bfloat16
AF = mybir.ActivationFunctionType
ALU = mybir.AluOpType
AX = mybir.AxisListType

CFG = dict(S=16384, DEPTH=4, B=2)
D = 2048
KC = 16
CTX = 256
DFF = 5632
HC = 44
NH = 8
EPS = 1e-6
SM_SCALE = 192.0 ** -0.5
ENGS = ("pe", "act", "dve", "pool", "sp")


class Res:
    __slots__ = ("w", "r")

    def __init__(self):
        self.w = None
        self.r = []


class Sched:
    def __init__(self, nc, stack):
        self.nc = nc
        self.stack = stack
        self.q = {e: [] for e in ENGS}
        self.seq = {e: 0 for e in ENGS}
        self.known = {e: {} for e in ENGS}
        self.sems = {}
        self.semval = {}
        self.reg = {}
        for e in ENGS:
            self.sems[e] = stack.enter_context(nc.semaphore("s_" + e))
            self.semval[e] = 0

    def dsem(self, key):
        if key not in self.sems:
            self.sems[key] = self.stack.enter_context(self.nc.semaphore("d_" + key))
            self.semval[key] = 0
            self.seq[key] = 0
        return key

    def _waits(self, eng, reads, writes):
        need = {}
        for r in reads:
            t = r.w
            if t is not None and need.get(t[0], 0) < t[1]:
                need[t[0]] = t[1]
        for w in writes:
            t = w.w
            if t is not None and t[0] != eng and need.get(t[0], 0) < t[1]:
                need[t[0]] = t[1]
            for t in w.r:
                if t[0] != eng and need.get(t[0], 0) < t[1]:
                    need[t[0]] = t[1]
        kn = self.known[eng]
        out = []
        for k, v in need.items():
            if k == eng and eng == "pe":
                continue
            if kn.get(k, 0) >= v:
                continue
            kn[k] = v
            self.reg[(k, v)][3] = True
            out.append((k, v))
        return out

    def _push(self, eng, key, fn, reads, writes, signal):
        waits = self._waits(eng, reads, writes)
        self.seq[key] += 1
        tok = (key, self.seq[key])
        ent = [waits, fn, key, signal]
        self.reg[tok] = ent
        self.q[eng].append((ent, tok))
        for r in reads:
            r.r.append(tok)
        for w in writes:
            w.w = tok
            w.r = []
        return tok

    def op(self, eng, fn, reads=(), writes=()):
        return self._push(eng, eng, fn, reads, writes, False)

    def dma(self, key, fn, reads=(), writes=(), eng="sp"):
        self.dsem(key)
        return self._push(eng, key, fn, reads, writes, True)

    def barrier(self):
        keys = list(self.sems)
        for e in ENGS:
            waits = []
            kn = self.known[e]
            for k in keys:
                v = self.seq[k]
                if k == e or v == 0 or kn.get(k, 0) >= v or (k, v) not in self.reg:
                    continue
                kn[k] = v
                self.reg[(k, v)][3] = True
                waits.append((k, v))
            self.seq[e] += 1
            tok = (e, self.seq[e])
            ent = [waits, (lambda en: en.nop()), e, True]
            self.reg[tok] = ent
            self.q[e].append((ent, tok))
        nop1 = {e: self.seq[e] for e in ENGS}
        for e in ENGS:
            waits = [(k, nop1[k]) for k in ENGS if k != e]
            self.seq[e] += 1
            tok = (e, self.seq[e])
            ent = [waits, (lambda en: en.nop()), e, False]
            self.reg[tok] = ent
            self.q[e].append((ent, tok))
        for e in ENGS:
            for k in keys:
                self.known[e][k] = self.seq[k]

    def flush(self):
        nc = self.nc
        val = {}
        for name in ENGS:
            for ent, tok in self.q[name]:
                if ent[3]:
                    k = ent[2]
                    self.semval[k] += 1 if k in ENGS else 16
                    val[tok] = self.semval[k]
        with nc.Block() as block:
            for name, deco in (("pe", block.tensor), ("act", block.scalar), ("dve", block.vector),
                               ("pool", block.gpsimd), ("sp", block.sync)):
                lst = self.q[name]
                if not lst:
                    continue

                def body(en, lst=lst):
                    sems = self.sems
                    for ent, tok in lst:
                        waits, fn, sk, sig = ent
                        for k, v in waits:
                            en.wait_ge(sems[k], val[(k, v)])
                        ins = fn(en)
                        if sig:
                            ins.then_inc(sems[sk], 1 if sk in ENGS else 16)
                deco(body)
        self.q = {e: [] for e in ENGS}
        self.reg = {}


class Builder:
    def __init__(self, S, DEPTH, debug=False):
        self.S = S
        self.DEPTH = DEPTH
        self.ST = S + CTX
        self.L1 = S // 128
        self.debug = debug
        self.nc = bass.Bass("TRN2", target_bir_lowering=False)
        self.res = {}

    def R(self, key):
        r = self.res.get(key)
        if r is None:
            r = self.res[key] = Res()
        return r

    def declare(self):
        nc, S, ST, L, L1 = self.nc, self.S, self.ST, self.DEPTH, self.L1
        di = lambda n, sh, dt=F32: nc.dram_tensor(n, sh, dt, kind="ExternalInput").ap()
        if self.debug:
            ds = lambda n, sh, dt=BF16: nc.dram_tensor(n, sh, dt, kind="ExternalOutput").ap()
        else:
            ds = lambda n, sh, dt=BF16: nc.dram_tensor(n, sh, dt).ap()
        self.xT = di("xT", [D, S])
        self.cxT = di("cxT", [D, CTX])
        self.cT = di("cT", [D, 2])
        self.w_ada = di("w_ada", [L * D, 144 * 128])
        self.b_adaT = di("b_adaT", [L * 128, 144])
        self.norm_gT = di("norm_gT", [L * 128, 48])
        self.wgu = di("wgu", [L * 2 * HC * 128, 2 * KC * 128])
        self.wd = di("wd", [L * 2 * KC * 128, HC * 128])
        self.win = di("win", [L * 15 * 128, KC * 128])
        self.wuq = di("wuq", [L * 128, 4 * 2048])
        self.wukv = di("wukv", [L * 128, 2 * 2048])
        self.wf = di("wf", [L * 128, 4 * 2 * 256])
        self.wo = di("wo", [L * 128, KC * 2048])
        self.gql = di("gql", [L * 128, 4])
        self.gkvl = di("gkvl", [L * 128, 2])
        self.gq = di("gq", [L * 128, 3])
        self.gk = di("gk", [L * 128, 3])
        self.gqb = di("gqb", [L * 128, 192])
        self.gkb = di("gkb", [L * 128, 192])
        self.ropec = di("ropec", [64, ST])
        self.ropes = di("ropes", [64, ST])
        self.fct = di("fct", [128, 2 * 512])
        self.t1a = di("t1a", [L1, 2 * L1])
        self.t1b = di("t1b", [L1, 2 * L1])
        self.er = di("er", [128, S])
        self.ei = di("ei", [128, S])
        self.t1ac = di("t1ac", [2, 4])
        self.t1bc = di("t1bc", [2, 4])
        self.erc = di("erc", [128, CTX])
        self.eic = di("eic", [128, CTX])
        self.yT = nc.dram_tensor("yT", [D, S], F32, kind="ExternalOutput").ap()
        self.xr = ds("xr", [D, ST], F32)
        self.wgu_b = ds("wgu_b", [2 * HC * 128, 2 * KC * 128])
        self.wd_b = ds("wd_b", [2 * KC * 128, HC * 128])
        self.win_b = ds("win_b", [15 * 128, KC * 128])
        self.QN = ds("QN", [NH * 128, ST])
        self.QR = ds("QR", [NH * 64, ST])
        self.KN = ds("KN", [NH * 128, ST])
        self.KR = ds("KR", [64, ST])
        self.V = ds("V", [ST, 1024])
        self.RK = ds("RK", [ST, 8], F32)
        self.GT = ds("GT", [4 * 512, ST])
        self.AT = ds("AT", [1024, ST])
        self.MT = ds("MT", [1024, ST])
        if self.debug:
            self.DBG_P = ds("DBG_P", [ST, 512])
            self.DBG_B = ds("DBG_B", [128, 4], F32)
            self.DBG_G = ds("DBG_G", [128, 192], F32)
            self.DBG_D = ds("DBG_D", [128, 512], F32)
            self.DBG_O = ds("DBG_O", [128, 512], F32)

    def sb(self, st, name, shape, dt=F32):
        self.uid = getattr(self, "uid", 0) + 1
        return st.enter_context(self.nc.sbuf_tensor("%s_%d" % (name, self.uid), shape, dt))

    def psum_banks(self, st, n=8):
        self.uid = getattr(self, "uid", 0) + 1
        return [st.enter_context(self.nc.psum_tensor("psb%d_%d" % (i, self.uid), [128, 512], F32)) for i in range(n)]

    def build(self):
        nc = self.nc
        self.declare()
        with ExitStack() as gst:
            self.sc = Sched(nc, gst)
            sc = self.sc
            self.ones_f = self.sb(gst, "ones_f", [128, 128], F32)
            self.ones_b = self.sb(gst, "ones_b", [128, 128], BF16)
            self.modT = self.sb(gst, "modT", [128, 144, 2], F32)
            self.Asc = self.sb(gst, "Asc", [128, 3, KC, 2], F32)
            self.Gat = self.sb(gst, "Gat", [128, 3, KC, 2], F32)
            self.gql_s = self.sb(gst, "gql_s", [128, 4], F32)
            self.gkvl_s = self.sb(gst, "gkvl_s", [128, 2], F32)
            self.gq_s = self.sb(gst, "gq_s", [128, 3], F32)
            self.gk_s = self.sb(gst, "gk_s", [128, 3], F32)
            self.negB = self.sb(gst, "negB", [128, 1], F32)
            sc.op("dve", lambda e: e.memset(self.ones_f[:], 1.0), writes=[self.R("ones_f")])
            sc.op("dve", lambda e: e.memset(self.ones_b[:], 1.0), writes=[self.R("ones_b")])
            self.copy_x_in()
            stop = self.debug if isinstance(self.debug, int) and not isinstance(self.debug, bool) else 99
            for l in range(self.DEPTH):
                last = l == self.DEPTH - 1
                self.phase_mod(l)
                self.phase_wconv(l)
                if stop >= 1:
                    self.phase_ffn(l, 0)
                if stop >= 2:
                    self.phase_mixA(l)
                if stop >= 3:
                    self.phase_attn(l, last)
                if stop >= 4:
                    self.phase_fourier(l, last)
                if stop >= 5:
                    self.phase_mixD(l, last)
                if stop >= 6:
                    self.phase_ffn(l, 1, last)
        return nc

    def copy_x_in(self):
        sc, S = self.sc, self.S
        with ExitStack() as st:
            buf = [self.sb(st, "cpb%d" % i, [128, KC, 512], F32) for i in range(2)]
            srcs = [(self.xT, t0, 512, t0) for t0 in range(0, S, 512)] + [(self.cxT, 0, CTX, S)]
            for i, (src, s0, T, d0) in enumerate(srcs):
                b = buf[i % 2]
                rb = self.R("cpb%d" % (i % 2))
                sv = src.rearrange("(c p) s -> p c s", p=128)[:, :, s0:s0 + T]
                dv = self.xr.rearrange("(c p) s -> p c s", p=128)[:, :, d0:d0 + T]
                sc.dma("cpl%d" % (i % 2), lambda e, b=b, sv=sv, T=T: e.dma_start(out=b[:, :, 0:T], in_=sv), writes=[rb])
                sc.dma("cps%d" % (i % 2), lambda e, b=b, dv=dv, T=T: e.dma_start(out=dv, in_=b[:, :, 0:T]), reads=[rb])
            sc.barrier()
            sc.flush()

    def phase_mod(self, l):
        sc, nc = self.sc, self.nc
        R = self.R
        with ExitStack() as st:
            ps = self.psum_banks(st, 2)
            cond = self.sb(st, "cond", [128, KC, 2], F32)
            scd = self.sb(st, "scd", [128, KC, 2], F32)
            bt = self.sb(st, "bt", [128, 144], F32)
            ng = self.sb(st, "ng", [128, 48], F32)
            gqb = self.sb(st, "gqb_s", [128, 192], F32)
            gkb = self.sb(st, "gkb_s", [128, 192], F32)
            mq = self.sb(st, "mq", [128, 1], F32)
            mk = self.sb(st, "mk", [128, 1], F32)
            wa = [self.sb(st, "wa%d" % i, [128, KC, 1024], F32) for i in range(2)]
            sc.dma("m_c", lambda e: e.dma_start(out=cond[:], in_=self.cT.rearrange("(c p) n -> p c n", p=128)), writes=[R("cond")])
            sc.dma("m_b", lambda e: e.dma_start(out=bt[:], in_=self.b_adaT[l * 128:(l + 1) * 128, :]), writes=[R("bt")])
            sc.dma("m_g", lambda e: e.dma_start(out=ng[:], in_=self.norm_gT[l * 128:(l + 1) * 128, :]), writes=[R("ng")])
            for nm, dst, src in (("gql", self.gql_s, self.gql), ("gkvl", self.gkvl_s, self.gkvl), ("gq", self.gq_s, self.gq),
                                 ("gk", self.gk_s, self.gk), ("gqb", gqb, self.gqb), ("gkb", gkb, self.gkb)):
                sc.dma("m_" + nm, lambda e, dst=dst, src=src: e.dma_start(out=dst[:], in_=src[l * 128:(l + 1) * 128, :]), writes=[R(nm)])
            sc.op("act", lambda e: e.activation(out=scd[:], in_=cond[:], func=AF.Silu), reads=[R("cond")], writes=[R("scd")])
            wav = self.w_ada[l * D:(l + 1) * D, :].rearrange("(c p) n -> p c n", p=128)
            for blk in range(18):
                w = wa[blk % 2]
                rw = R("wa%d" % (blk % 2))
                for half in range(2):
                    sc.dma("m_w%d" % (blk % 2), lambda e, w=w, blk=blk, half=half: e.dma_start(
                        out=w[:, half * 8:(half + 1) * 8, :], in_=wav[:, half * 8:(half + 1) * 8, blk * 1024:(blk + 1) * 1024]), writes=[rw])
                for jj in range(8):
                    j = blk * 8 + jj
                    for kc in range(KC):
                        sc.op("pe", lambda e, w=w, jj=jj, kc=kc, j=j: e.matmul(
                            ps[0][:, 2 * j:2 * j + 2], w[:, kc, jj * 128:(jj + 1) * 128], scd[:, kc, :],
                            start=(kc == 0), stop=(kc == KC - 1)), reads=[rw, R("scd")], writes=[R("mps")])
            for n in range(2):
                sc.op("dve", lambda e, n=n: e.tensor_tensor(
                    out=self.modT[:, :, n], in0=ps[0][:, 0:288].rearrange("p (j n) -> p j n", n=2)[:, :, n], in1=bt[:], op=ALU.add),
                    reads=[R("mps"), R("bt")], writes=[R("modT")])
            for s in range(3):
                for n in range(2):
                    sc.op("dve", lambda e, s=s, n=n: e.scalar_tensor_tensor(
                        out=self.Asc[:, s, :, n], in0=self.modT[:, (3 * s + 1) * KC:(3 * s + 2) * KC, n], scalar=1.0,
                        in1=ng[:, s * KC:(s + 1) * KC], op0=ALU.add, op1=ALU.mult),
                        reads=[R("modT"), R("ng")], writes=[R("Asc")])
                    sc.op("dve", lambda e, s=s, n=n: e.tensor_scalar(
                        out=self.Gat[:, s, :, n], in0=self.modT[:, (3 * s + 2) * KC:(3 * s + 3) * KC, n],
                        scalar1=(1.0 if s == 1 else 0.5), scalar2=None, op0=ALU.mult),
                        reads=[R("modT")], writes=[R("Gat")])
            sc.op("dve", lambda e: e.tensor_reduce(out=mq[:], in_=gqb[:], axis=AX.X, op=ALU.max, apply_absolute_value=True),
                  reads=[R("gqb")], writes=[R("mq")])
            sc.op("dve", lambda e: e.tensor_reduce(out=mk[:], in_=gkb[:], axis=AX.X, op=ALU.max, apply_absolute_value=True),
                  reads=[R("gkb")], writes=[R("mk")])
            sc.op("dve", lambda e: e.tensor_scalar(out=self.negB[:], in0=mq[:], scalar1=mk[:, 0:1], scalar2=-SM_SCALE * 192.0,
                                                   op0=ALU.mult, op1=ALU.mult),
                  reads=[R("mq"), R("mk")], writes=[R("negB")])
            if self.debug:
                sc.dma("dbgb0", lambda e: e.dma_start(out=self.DBG_B[:, 0:1], in_=mq[:], allow_slow_non_contiguous=True), reads=[R("mq")])
                sc.dma("dbgb1", lambda e: e.dma_start(out=self.DBG_B[:, 1:2], in_=mk[:], allow_slow_non_contiguous=True), reads=[R("mk")])
                sc.dma("dbgb2", lambda e: e.dma_start(out=self.DBG_B[:, 2:3], in_=self.negB[:], allow_slow_non_contiguous=True), reads=[R("negB")])
                sc.dma("dbgb3", lambda e: e.dma_start(out=self.DBG_G[:, :], in_=gqb[:]), reads=[R("gqb")])
            sc.barrier()
            sc.flush()

    def phase_wconv(self, l):
        sc = self.sc
        R = self.R
        with ExitStack() as st:
            NB = 3
            fb = [self.sb(st, "wcf%d" % i, [128, 8192], F32) for i in range(NB)]
            bb = [self.sb(st, "wcb%d" % i, [128, 8192], BF16) for i in range(NB)]
            jobs = []
            def add(src, dst, row0, nblk, F):
                for b in range(nblk):
                    for f0 in range(0, F, 8192):
                        fl = min(8192, F - f0)
                        jobs.append((src[row0 + b * 128:row0 + (b + 1) * 128, f0:f0 + fl], dst[b * 128:(b + 1) * 128, f0:f0 + fl], fl))
            add(self.wgu, self.wgu_b, l * 2 * HC * 128, 2 * HC, 2 * KC * 128)
            add(self.wd, self.wd_b, l * 2 * KC * 128, 2 * KC, HC * 128)
            add(self.win, self.win_b, l * 15 * 128, 15, KC * 128)
            engs = ("dve", "act", "pool")
            for i, (sv, dv, fl) in enumerate(jobs):
                s = i % NB
                rf, rb = R("wcf%d" % s), R("wcb%d" % s)
                sc.dma("wcl%d" % s, lambda e, s=s, sv=sv, fl=fl: e.dma_start(out=fb[s][:, 0:fl], in_=sv), writes=[rf])
                en = engs[i % 3]
                if en == "act":
                    sc.op("act", lambda e, s=s, fl=fl: e.copy(out=bb[s][:, 0:fl], in_=fb[s][:, 0:fl]), reads=[rf], writes=[rb])
                else:
                    sc.op(en, lambda e, s=s, fl=fl: e.tensor_copy(out=bb[s][:, 0:fl], in_=fb[s][:, 0:fl]), reads=[rf], writes=[rb])
                sc.dma("wcs%d" % s, lambda e, s=s, dv=dv, fl=fl: e.dma_start(out=dv, in_=bb[s][:, 0:fl]), reads=[rb])
            sc.barrier()
            sc.flush()

    def emit_norm_h(self, X, rX, h, rh, T, s, col, ps_ss, tmp):
        sc, R = self.sc, self.R
        sq, rstd, t2 = tmp["sq"], tmp["rstd"], tmp["t2"]
        for c in range(KC):
            k = c % 2
            sc.op("dve", lambda e, c=c, k=k: e.tensor_tensor(out=sq[k][:, 0:T], in0=X[:, c, 0:T], in1=X[:, c, 0:T], op=ALU.mult),
                  reads=[rX], writes=[R("sq%d" % k)])
            sc.op("pe", lambda e, c=c, k=k: e.matmul(ps_ss[:, 0:T], self.ones_f[:], sq[k][:, 0:T], start=(c == 0), stop=(c == KC - 1)),
                  reads=[R("sq%d" % k), R("ones_f")], writes=[R("ps_ss")])
        sc.op("act", lambda e: e.activation(out=rstd[:, 0:T], in_=ps_ss[:, 0:T], func=AF.Sqrt, bias=self.eps_t[:], scale=1.0 / D),
              reads=[R("ps_ss"), R("eps_t")], writes=[R("rstd")])
        sc.op("dve", lambda e: e.reciprocal(out=rstd[:, 0:T], in_=rstd[:, 0:T]), reads=[R("rstd")], writes=[R("rstd")])
        for c in range(KC):
            k = c % 2
            sc.op("dve", lambda e, c=c, k=k: e.scalar_tensor_tensor(
                out=t2[k][:, 0:T], in0=X[:, c, 0:T], scalar=self.Asc[:, s, c, col:col + 1], in1=rstd[:, 0:T],
                op0=ALU.mult, op1=ALU.mult), reads=[rX, R("rstd"), R("Asc")], writes=[R("t2%d" % k)])
            sc.op("act", lambda e, c=c, k=k: e.activation(
                out=h[:, c, 0:T], in_=t2[k][:, 0:T], func=AF.Identity, bias=self.modT[:, 3 * s * KC + c, col:col + 1], scale=1.0),
                reads=[R("t2%d" % k), R("modT")], writes=[rh])

    def alloc_norm_tmp(self, st):
        tmp = dict(sq=[self.sb(st, "sq%d" % i, [128, 512], F32) for i in range(2)],
                   t2=[self.sb(st, "t2%d" % i, [128, 512], F32) for i in range(2)],
                   rstd=self.sb(st, "rstd", [128, 512], F32))
        self.eps_t = self.sb(st, "eps_t", [128, 1], F32)
        self.sc.op("dve", lambda e: e.memset(self.eps_t[:], EPS), writes=[self.R("eps_t")])
        return tmp

    def tiles(self, with_ctx=True):
        t = [(t0, 512, 0) for t0 in range(0, self.S, 512)]
        if with_ctx:
            t.append((self.S, CTX, 1))
        return t

    def phase_ffn(self, l, which, last=False):
        sc, R = self.sc, self.R
        s = 0 if which == 0 else 2
        final = last and which == 1
        tl = self.tiles(with_ctx=not final)
        xv = self.xr.rearrange("(c p) s -> p c s", p=128)
        yv = self.yT.rearrange("(c p) s -> p c s", p=128)
        with ExitStack() as st:
            ps = self.psum_banks(st, 8)
            X = [self.sb(st, "X%d" % i, [128, KC, 512], F32) for i in range(2)]
            h = self.sb(st, "h", [128, KC, 512], BF16)
            act = self.sb(st, "actb", [128, HC, 512], BF16)
            NG = 3
            gub = [self.sb(st, "gub%d" % i, [128, 2, KC, 128], BF16) for i in range(NG)]
            db = [self.sb(st, "db%d" % i, [128, HC, 128], BF16) for i in range(2)]
            sg = [self.sb(st, "sg%d" % i, [128, 512], F32) for i in range(2)]
            tmp = self.alloc_norm_tmp(st)
            ps_ss, ps_g, ps_u, ps_o = ps[0], ps[1:3], ps[3:5], ps[5:7]
            guv = self.wgu_b[which * HC * 128:(which + 1) * HC * 128, :]
            dv_ = self.wd_b[which * KC * 128:(which + 1) * KC * 128, :]

            def load_x(i):
                t0, T, col = tl[i]
                sc.dma("xl%d" % (i % 2), lambda e: e.dma_start(out=X[i % 2][:, :, 0:T], in_=xv[:, :, t0:t0 + T]), writes=[R("X%d" % (i % 2))])

            def norm(i):
                t0, T, col = tl[i]
                self.emit_norm_h(X[i % 2], R("X%d" % (i % 2)), h, R("h"), T, s, col, ps_ss, tmp)

            load_x(0)
            norm(0)
            gi = 0
            di_ = 0
            def tile_body(i, t0, T, col):
                nonlocal gi, di_
                Xi, rX = X[i % 2], R("X%d" % (i % 2))
                if i + 1 < len(tl):
                    load_x(i + 1)
                for j in range(HC):
                    g = gi % NG
                    gi += 1
                    rg = R("gub%d" % g)
                    sc.dma("gul%d" % g, lambda e, g=g, j=j: e.dma_start(
                        out=gub[g][:], in_=guv[j * 128:(j + 1) * 128, :].rearrange("p (a c m) -> p a c m", a=2, c=KC)), writes=[rg])
                    k = j % 2
                    for a, pst, nm in ((0, ps_g[k], "psg%d" % k), (1, ps_u[k], "psu%d" % k)):
                        for c in range(KC):
                            sc.op("pe", lambda e, g=g, a=a, c=c, pst=pst: e.matmul(
                                pst[:, 0:T], gub[g][:, a, c, :], h[:, c, 0:T], start=(c == 0), stop=(c == KC - 1)),
                                reads=[rg, R("h")], writes=[R(nm)])
                    sc.op("act", lambda e, k=k: e.activation(out=sg[k][:, 0:T], in_=ps_g[k][:, 0:T], func=AF.Silu),
                          reads=[R("psg%d" % k)], writes=[R("sg%d" % k)])
                    sc.op("dve", lambda e, k=k, j=j: e.tensor_tensor(out=act[:, j, 0:T], in0=sg[k][:, 0:T], in1=ps_u[k][:, 0:T], op=ALU.mult),
                          reads=[R("sg%d" % k), R("psu%d" % k)], writes=[R("act")])
                if i + 1 < len(tl):
                    norm(i + 1)
                for j in range(KC):
                    d = di_ % 2
                    di_ += 1
                    rd = R("db%d" % d)
                    sc.dma("dl%d" % d, lambda e, d=d, j=j: e.dma_start(
                        out=db[d][:], in_=dv_[j * 128:(j + 1) * 128, :].rearrange("p (c m) -> p c m", c=HC)), writes=[rd])
                    k = j % 2
                    for c in range(HC):
                        sc.op("pe", lambda e, d=d, c=c, k=k: e.matmul(
                            ps_o[k][:, 0:T], db[d][:, c, :], act[:, c, 0:T], start=(c == 0), stop=(c == HC - 1)),
                            reads=[rd, R("act")], writes=[R("pso%d" % k)])
                    sc.op("dve", lambda e, j=j, k=k, Xi=Xi: e.scalar_tensor_tensor(
                        out=Xi[:, j, 0:T], in0=ps_o[k][:, 0:T], scalar=self.Gat[:, s, j, col:col + 1], in1=Xi[:, j, 0:T],
                        op0=ALU.mult, op1=ALU.add), reads=[R("pso%d" % k), R("Gat"), rX], writes=[rX])
                dst = yv if final else xv
                sc.dma("xs%d" % (i % 2), lambda e, Xi=Xi, dst=dst: e.dma_start(out=dst[:, :, t0:t0 + T], in_=Xi[:, :, 0:T]), reads=[rX])
            for i, (t0, T, col) in enumerate(tl):
                tile_body(i, t0, T, col)
            sc.barrier()
            sc.flush()

    def load_cast(self, st_tmp, dst, dst_res, src, F, tag):
        sc, R = self.sc, self.R
        stg = st_tmp
        engs = ("dve", "pool")
        CHK = stg[0].shape[-1]
        for i, f0 in enumerate(range(0, F, CHK)):
            fl = min(CHK, F - f0)
            k = i % 2
            rs = R("lcs%d" % k)
            sc.dma("lc%d" % k, lambda e, k=k, f0=f0, fl=fl: e.dma_start(out=stg[k][:, 0:fl], in_=src[:, f0:f0 + fl]), writes=[rs])
            sc.op(engs[i % 2], lambda e, k=k, f0=f0, fl=fl: e.tensor_copy(out=dst[:, f0:f0 + fl], in_=stg[k][:, 0:fl]),
                  reads=[rs], writes=[dst_res])

    def phase_mixA(self, l):
        sc, R = self.sc, self.R
        S, ST = self.S, self.ST
        tl = [(t0, 256, 0) for t0 in range(0, S, 256)] + [(S, CTX, 1)]
        xv = self.xr.rearrange("(c p) s -> p c s", p=128)
        with ExitStack() as st:
            ps = self.psum_banks(st, 8)
            stg = [self.sb(st, "stg%d" % i, [128, 2048], F32) for i in range(2)]
            Wuq = self.sb(st, "Wuq", [128, 4 * 2048], BF16)
            Wukv = self.sb(st, "Wukv", [128, 2 * 2048], BF16)
            FC = self.sb(st, "FC", [128, 2 * 512], BF16)
            self.load_cast(stg, Wuq, R("Wuq"), self.wuq[l * 128:(l + 1) * 128, :], 4 * 2048, "uq")
            self.load_cast(stg, Wukv, R("Wukv"), self.wukv[l * 128:(l + 1) * 128, :], 2 * 2048, "ukv")
            self.load_cast(stg, FC, R("FC"), self.fct, 1024, "fc")
            Wuq3 = Wuq[:].rearrange("p (c n) -> p c n", c=4)
            Wukv3 = Wukv[:].rearrange("p (c n) -> p c n", c=2)
            FC3 = FC[:].rearrange("p (c n) -> p c n", c=2)
            X = self.sb(st, "X", [128, KC, 256], F32)
            h = self.sb(st, "h", [128, KC, 256], BF16)
            wsl = [self.sb(st, "wsl%d" % i, [128, KC, 128], BF16) for i in range(3)]
            ql = self.sb(st, "ql", [128, 4, 256], F32)
            qn = self.sb(st, "qn", [128, 4, 256], BF16)
            kvl = self.sb(st, "kvl", [128, 2, 256], F32)
            kvn = self.sb(st, "kvn", [128, 2, 256], BF16)
            krr = self.sb(st, "krr", [64, 2, 256], F32)
            fT = self.sb(st, "fT", [128, 8, 256], BF16)
            QNs = self.sb(st, "QNs", [128, NH, 256], BF16)
            QRs = self.sb(st, "QRs", [64, NH, 256], BF16)
            KNs = self.sb(st, "KNs", [128, NH, 256], BF16)
            KRs = self.sb(st, "KRs", [64, 256], BF16)
            Vs = self.sb(st, "Vs", [128, 4, 1024], BF16)
            RKs = self.sb(st, "RKs", [128, 4, 8], F32)
            GTs = self.sb(st, "GTs", [128, 16, 256], BF16)
            rc = self.sb(st, "rc", [64, 256], F32)
            rs_ = self.sb(st, "rs", [64, 256], F32)
            sqk = [self.sb(st, "sqk%d" % i, [128, 256], F32) for i in range(2)]
            sqr = [self.sb(st, "sqr%d" % i, [64, 256], F32) for i in range(2)]
            rq = [self.sb(st, "rq%d" % i, [128, 256], F32) for i in range(2)]
            ra = [self.sb(st, "ra%d" % i, [64, 256], F32) for i in range(2)]
            rb = [self.sb(st, "rb%d" % i, [64, 256], F32) for i in range(2)]
            e192 = self.sb(st, "e192", [128, 1], F32)
            sqkb = [self.sb(st, "sqkb%d" % i, [128, 256], BF16) for i in range(2)]
            sqrb = self.sb(st, "sqrb", [128, 256], BF16)
            sc.op("pool", lambda e: e.memset(sqrb[:], 0.0), writes=[R("sqrb")])
            tmp = self.alloc_norm_tmp(st)
            sc.op("dve", lambda e: e.memset(e192[:], EPS), writes=[R("e192")])
            nps = [0]

            def bank():
                b = nps[0] % 6 + 2
                nps[0] += 1
                return ps[b], R("psb%d" % b)
            ps_ss = ps[0]
            ps_rk, r_rk = ps[1], R("psb1")
            wi = [0]
            winv = self.win_b

            def load_slab(j):
                k = wi[0] % 3
                wi[0] += 1
                sc.dma("wsl%d" % k, lambda e: e.dma_start(
                    out=wsl[k][:], in_=winv[j * 128:(j + 1) * 128, :].rearrange("p (c m) -> p c m", c=KC)), writes=[R("wsl%d" % k)])
                return wsl[k], R("wsl%d" % k)

            def tile_body(i, t0, T, col):
                nb = T // 128
                sc.dma("xl", lambda e: e.dma_start(out=X[:, :, 0:T], in_=xv[:, :, t0:t0 + T]), writes=[R("X")])
                sc.dma("rcl", lambda e: e.dma_start(out=rc[:, 0:T], in_=self.ropec[:, t0:t0 + T]), writes=[R("rc")])
                sc.dma("rsl", lambda e: e.dma_start(out=rs_[:, 0:T], in_=self.ropes[:, t0:t0 + T]), writes=[R("rs")])
                self.emit_norm_h(X, R("X"), h, R("h"), T, 1, col, ps_ss, tmp)
                for j in range(15):
                    w, rw = load_slab(j)
                    if j == 6:
                        for half in range(2):
                            pb, rpb = bank()
                            for c in range(KC):
                                sc.op("pe", lambda e, w=w, c=c, pb=pb, half=half: e.matmul(
                                    pb[0:64, 0:T], w[:, c, half * 64:(half + 1) * 64], h[:, c, 0:T], start=(c == 0), stop=(c == KC - 1)),
                                    reads=[rw, R("h")], writes=[rpb])
                            sc.op("act", lambda e, pb=pb, half=half: e.copy(out=krr[:, half, 0:T], in_=pb[0:64, 0:T]), reads=[rpb], writes=[R("krr")])
                        continue
                    pb, rpb = bank()
                    for c in range(KC):
                        sc.op("pe", lambda e, w=w, c=c, pb=pb: e.matmul(pb[:, 0:T], w[:, c, :], h[:, c, 0:T], start=(c == 0), stop=(c == KC - 1)),
                              reads=[rw, R("h")], writes=[rpb])
                    if j < 4:
                        sc.op("act", lambda e, pb=pb, j=j: e.copy(out=ql[:, j, 0:T], in_=pb[:, 0:T]), reads=[rpb], writes=[R("ql")])
                    elif j < 6:
                        sc.op("act", lambda e, pb=pb, j=j: e.copy(out=kvl[:, j - 4, 0:T], in_=pb[:, 0:T]), reads=[rpb], writes=[R("kvl")])
                    else:
                        sc.op("act", lambda e, pb=pb, j=j: e.copy(out=fT[:, j - 7, 0:T], in_=pb[:, 0:T]), reads=[rpb], writes=[R("fT")])
                for (src, rsrc, nchunk, dstn, rdst, gain, rg) in ((ql, R("ql"), 4, qn, R("qn"), self.gql_s, R("gql")),
                                                               (kvl, R("kvl"), 2, kvn, R("kvn"), self.gkvl_s, R("gkvl"))):
                    pb, rpb = bank()
                    for c in range(nchunk):
                        k = c % 2
                        sc.op("pool", lambda e, src=src, c=c, k=k: e.tensor_tensor(out=sqk[k][:, 0:T], in0=src[:, c, 0:T], in1=src[:, c, 0:T], op=ALU.mult),
                              reads=[rsrc], writes=[R("sqk%d" % k)])
                        sc.op("pe", lambda e, c=c, k=k, pb=pb, nchunk=nchunk: e.matmul(pb[:, 0:T], self.ones_f[:], sqk[k][:, 0:T], start=(c == 0), stop=(c == nchunk - 1)),
                              reads=[R("sqk%d" % k), R("ones_f")], writes=[rpb])
                    sc.op("act", lambda e, pb=pb, nchunk=nchunk: e.activation(out=rq[0][:, 0:T], in_=pb[:, 0:T], func=AF.Sqrt, bias=e192[:], scale=1.0 / (128 * nchunk)),
                          reads=[rpb, R("e192")], writes=[R("rq0")])
                    sc.op("dve", lambda e: e.reciprocal(out=rq[0][:, 0:T], in_=rq[0][:, 0:T]), reads=[R("rq0")], writes=[R("rq0")])
                    for c in range(nchunk):
                        sc.op("dve", lambda e, src=src, c=c, dstn=dstn, gain=gain: e.scalar_tensor_tensor(
                            out=dstn[:, c, 0:T], in0=src[:, c, 0:T], scalar=gain[:, c:c + 1], in1=rq[0][:, 0:T], op0=ALU.mult, op1=ALU.mult),
                            reads=[rsrc, R("rq0"), rg], writes=[rdst])
                sc.op("dve", lambda e: e.scalar_tensor_tensor(out=ra[0][:, 0:T], in0=krr[:, 0, 0:T], scalar=self.gk_s[0:64, 1:2], in1=rc[:, 0:T],
                                                              op0=ALU.mult, op1=ALU.mult), reads=[R("krr"), R("gk"), R("rc")], writes=[R("ra0")])
                sc.op("dve", lambda e: e.scalar_tensor_tensor(out=rb[0][:, 0:T], in0=krr[:, 1, 0:T], scalar=self.gk_s[0:64, 2:3], in1=rs_[:, 0:T],
                                                              op0=ALU.mult, op1=ALU.mult), reads=[R("krr"), R("gk"), R("rs")], writes=[R("rb0")])
                sc.op("dve", lambda e: e.tensor_tensor(out=KRs[:, 0:T], in0=ra[0][:, 0:T], in1=rb[0][:, 0:T], op=ALU.add),
                      reads=[R("ra0"), R("rb0")], writes=[R("KRs")])
                sc.dma("krs", lambda e: e.dma_start(out=self.KR[:, t0:t0 + T], in_=KRs[:, 0:T]), reads=[R("KRs")])
                sc.op("pool", lambda e: e.tensor_tensor(out=sqrb[0:64, 0:T], in0=krr[:, 0, 0:T], in1=krr[:, 0, 0:T], op=ALU.mult),
                      reads=[R("krr")], writes=[R("sqrb")])
                for hd in range(NH):
                    k = hd % 2
                    pb, rpb = bank()
                    for c in range(2):
                        sc.op("pe", lambda e, c=c, pb=pb, hd=hd: e.matmul(pb[:, 0:T], Wukv3[:, c, hd * 128:(hd + 1) * 128], kvn[:, c, 0:T],
                                                                        start=(c == 0), stop=(c == 1)), reads=[R("Wukv"), R("kvn")], writes=[rpb])
                    sc.op("act", lambda e, pb=pb, hd=hd: e.activation(out=KNs[:, hd, 0:T], in_=pb[:, 0:T], func=AF.Identity, scale=self.gk_s[:, 0:1]),
                          reads=[rpb, R("gk")], writes=[R("KNs")])
                    sc.op("act", lambda e, pb=pb, k=k: e.activation(out=sqkb[k][:, 0:T], in_=pb[:, 0:T], func=AF.Square),
                          reads=[rpb], writes=[R("sqkb%d" % k)])
                    for b in range(nb):
                        o2 = (b * 8 + hd) * 2
                        sc.op("pe", lambda e, b=b, k=k, o2=o2: e.matmul(ps_rk[:, o2:o2 + 2], sqkb[k][:, b * 128:(b + 1) * 128], self.ones_b[:, 0:2],
                                                                       start=True, stop=False), reads=[R("sqkb%d" % k), R("ones_b")], writes=[r_rk])
                        sc.op("pe", lambda e, b=b, o2=o2: e.matmul(ps_rk[:, o2:o2 + 2], sqrb[:, b * 128:(b + 1) * 128], self.ones_b[:, 0:2],
                                                                  start=False, stop=True), reads=[R("sqrb"), R("ones_b")], writes=[r_rk])
                    pn, rpn = bank()
                    for c in range(4):
                        sc.op("pe", lambda e, c=c, pn=pn, hd=hd: e.matmul(pn[:, 0:T], Wuq3[:, c, hd * 128:(hd + 1) * 128], qn[:, c, 0:T],
                                                                        start=(c == 0), stop=(c == 3)), reads=[R("Wuq"), R("qn")], writes=[rpn])
                    pr, rpr = bank()
                    for c in range(4):
                        sc.op("pe", lambda e, c=c, pr=pr, hd=hd: e.matmul(pr[0:64, 0:T], Wuq3[:, c, 1024 + hd * 64:1024 + (hd + 1) * 64], qn[:, c, 0:T],
                                                                        start=(c == 0), stop=(c == 3)), reads=[R("Wuq"), R("qn")], writes=[rpr])
                    pw, rpw = bank()
                    for c in range(4):
                        sc.op("pe", lambda e, c=c, pw=pw, hd=hd: e.matmul(pw[0:64, 0:T], Wuq3[:, c, 1536 + hd * 64:1536 + (hd + 1) * 64], qn[:, c, 0:T],
                                                                        start=(c == 0), stop=(c == 3)), reads=[R("Wuq"), R("qn")], writes=[rpw])
                    k2 = 1 - k
                    sc.op("act", lambda e, pn=pn, k2=k2: e.activation(out=sqk[k2][:, 0:T], in_=pn[:, 0:T], func=AF.Square),
                          reads=[rpn], writes=[R("sqk%d" % k2)])
                    sc.op("act", lambda e, pr=pr: e.activation(out=sqr[1][:, 0:T], in_=pr[0:64, 0:T], func=AF.Square),
                          reads=[rpr], writes=[R("sqr1")])
                    pq, rpq = bank()
                    sc.op("pe", lambda e, pq=pq, k2=k2: e.matmul(pq[:, 0:T], self.ones_f[:], sqk[k2][:, 0:T], start=True, stop=False),
                          reads=[R("sqk%d" % k2), R("ones_f")], writes=[rpq])
                    sc.op("pe", lambda e, pq=pq: e.matmul(pq[:, 0:T], self.ones_f[0:64, :], sqr[1][:, 0:T], start=False, stop=True),
                          reads=[R("sqr1"), R("ones_f")], writes=[rpq])
                    sc.op("act", lambda e, pq=pq: e.activation(out=rq[1][:, 0:T], in_=pq[:, 0:T], func=AF.Sqrt, bias=e192[:], scale=1.0 / 192.0),
                          reads=[rpq, R("e192")], writes=[R("rq1")])
                    sc.op("dve", lambda e: e.reciprocal(out=rq[1][:, 0:T], in_=rq[1][:, 0:T]), reads=[R("rq1")], writes=[R("rq1")])
                    sc.op("dve", lambda e, pn=pn, hd=hd: e.scalar_tensor_tensor(out=QNs[:, hd, 0:T], in0=pn[:, 0:T], scalar=self.gq_s[:, 0:1], in1=rq[1][:, 0:T],
                                                                              op0=ALU.mult, op1=ALU.mult), reads=[rpn, R("gq"), R("rq1")], writes=[R("QNs")])
                    sc.op("dve", lambda e, pr=pr: e.scalar_tensor_tensor(out=ra[1][:, 0:T], in0=pr[0:64, 0:T], scalar=self.gq_s[0:64, 1:2], in1=rc[:, 0:T],
                                                                       op0=ALU.mult, op1=ALU.mult), reads=[rpr, R("gq"), R("rc")], writes=[R("ra1")])
                    sc.op("dve", lambda e, pw=pw: e.scalar_tensor_tensor(out=rb[1][:, 0:T], in0=pw[0:64, 0:T], scalar=self.gq_s[0:64, 2:3], in1=rs_[:, 0:T],
                                                                       op0=ALU.mult, op1=ALU.mult), reads=[rpw, R("gq"), R("rs")], writes=[R("rb1")])
                    sc.op("pool", lambda e: e.tensor_tensor(out=ra[1][:, 0:T], in0=ra[1][:, 0:T], in1=rb[1][:, 0:T], op=ALU.add),
                          reads=[R("ra1"), R("rb1")], writes=[R("ra1")])
                    sc.op("pool", lambda e, hd=hd: e.tensor_tensor(out=QRs[:, hd, 0:T], in0=ra[1][:, 0:T], in1=rq[1][0:64, 0:T], op=ALU.mult),
                          reads=[R("ra1"), R("rq1")], writes=[R("QRs")])
                sc.op("act", lambda e: e.activation(out=RKs[:, 0:nb, :], in_=ps_rk[:, 0:nb * 16].rearrange("p (b h two) -> p b h two", h=8, two=2)[:, :, :, 0], func=AF.Sqrt,
                                                    bias=e192[:], scale=1.0 / 192.0), reads=[r_rk, R("e192")], writes=[R("RKs")])
                sc.op("dve", lambda e: e.reciprocal(out=RKs[:, 0:nb, :], in_=RKs[:, 0:nb, :]), reads=[R("RKs")], writes=[R("RKs")])
                sc.op("dve", lambda e: e.tensor_scalar(out=RKs[:, 0:nb, :], in0=RKs[:, 0:nb, :], scalar1=SM_SCALE, scalar2=None, op0=ALU.mult),
                      reads=[R("RKs")], writes=[R("RKs")])
                sc.dma("rks", lambda e: e.dma_start(out=self.RK[t0:t0 + T, :].rearrange("(b p) h -> p b h", p=128), in_=RKs[:, 0:nb, :]), reads=[R("RKs")])
                sc.dma("qns", lambda e: e.dma_start(out=self.QN.rearrange("(h p) s -> p h s", p=128)[:, :, t0:t0 + T], in_=QNs[:, :, 0:T]), reads=[R("QNs")])
                sc.dma("qrs", lambda e: e.dma_start(out=self.QR.rearrange("(h p) s -> p h s", p=64)[:, :, t0:t0 + T], in_=QRs[:, :, 0:T]), reads=[R("QRs")])
                sc.dma("kns", lambda e: e.dma_start(out=self.KN.rearrange("(h p) s -> p h s", p=128)[:, :, t0:t0 + T], in_=KNs[:, :, 0:T]), reads=[R("KNs")])
                for b in range(nb):
                    for hh in range(2):
                        pb, rpb = bank()
                        for c in range(2):
                            sc.op("pe", lambda e, b=b, c=c, hh=hh, pb=pb: e.matmul(pb[:, 0:512], kvn[:, c, b * 128:(b + 1) * 128],
                                                                                 Wukv3[:, c, 1024 + hh * 512:1024 + (hh + 1) * 512], start=(c == 0), stop=(c == 1)),
                                  reads=[R("kvn"), R("Wukv")], writes=[rpb])
                        sc.op("act", lambda e, b=b, hh=hh, pb=pb: e.copy(out=Vs[:, b, hh * 512:(hh + 1) * 512], in_=pb[:, 0:512]), reads=[rpb], writes=[R("Vs")])
                sc.dma("vs", lambda e: e.dma_start(out=self.V[t0:t0 + T, :].rearrange("(b p) n -> p b n", p=128), in_=Vs[:, 0:nb, :]), reads=[R("Vs")])
                for g in range(4):
                    for xc in range(4):
                        pb, rpb = bank()
                        for c in range(2):
                            sc.op("pe", lambda e, g=g, xc=xc, c=c, pb=pb: e.matmul(pb[:, 0:T], FC3[:, c, xc * 128:(xc + 1) * 128], fT[:, 2 * g + c, 0:T],
                                                                                 start=(c == 0), stop=(c == 1)), reads=[R("FC"), R("fT")], writes=[rpb])
                        eng = "act" if (xc % 2 == 0) else "dve"
                        if eng == "act":
                            sc.op("act", lambda e, g=g, xc=xc, pb=pb: e.copy(out=GTs[:, g * 4 + xc, 0:T], in_=pb[:, 0:T]), reads=[rpb], writes=[R("GTs")])
                        else:
                            sc.op("dve", lambda e, g=g, xc=xc, pb=pb: e.tensor_copy(out=GTs[:, g * 4 + xc, 0:T], in_=pb[:, 0:T]), reads=[rpb], writes=[R("GTs")])
                sc.dma("gts", lambda e: e.dma_start(out=self.GT.rearrange("(c p) s -> p c s", p=128)[:, :, t0:t0 + T], in_=GTs[:, :, 0:T]), reads=[R("GTs")])
            for i, (t0, T, col) in enumerate(tl):
                tile_body(i, t0, T, col)
            sc.barrier()
            sc.flush()

    def phase_attn(self, l, last):
        sc, R = self.sc, self.R
        S, ST = self.S, self.ST
        NBL, NBT = S // 128, ST // 128
        with ExitStack() as st:
            ps = self.psum_banks(st, 8)
            KNh = self.sb(st, "KNh", [128, ST], BF16)
            KRt = self.sb(st, "KRt", [128, ST], BF16)
            Vh = self.sb(st, "Vh", [128, NBT, 128], BF16)
            RKa = self.sb(st, "RKa", [128, NBT, 8], F32)
            qn = [self.sb(st, "aqn%d" % i, [128, 512], BF16) for i in range(2)]
            qr = [self.sb(st, "aqr%d" % i, [128, 512], BF16) for i in range(2)]
            pT = [self.sb(st, "pT%d" % i, [128, 512], BF16) for i in range(4)]
            rden = [self.sb(st, "rden%d" % i, [128, 512], F32) for i in range(2)]
            ot = [self.sb(st, "ot%d" % i, [128, 512], BF16) for i in range(2)]
            self.dbo = self.sb(st, "dbo", [128, 512], F32)
            sc.op("dve", lambda e: e.memset(KRt[:], 0.0), writes=[R("KRt")])
            for i in range(2):
                sc.op("dve", lambda e, i=i: e.memset(qr[i][:], 0.0), writes=[R("aqr%d" % i)])
            CH = 4096
            for c0 in range(0, ST, CH):
                cl = min(CH, ST - c0)
                sc.dma("krl", lambda e, c0=c0, cl=cl: e.dma_start(out=KRt[0:64, c0:c0 + cl], in_=self.KR[:, c0:c0 + cl]), writes=[R("KRt")])
            for b0 in range(0, NBT, 32):
                bl = min(32, NBT - b0)
                sc.dma("rkl", lambda e, b0=b0, bl=bl: e.dma_start(
                    out=RKa[:, b0:b0 + bl, :], in_=self.RK[b0 * 128:(b0 + bl) * 128, :].rearrange("(b p) h -> p b h", p=128)), writes=[R("RKa")])
            qi = [0]

            def qtile(hd, t0, T, kbs):
                i = qi[0]
                qi[0] += 1
                s2 = i % 2
                rqn, rqr = R("aqn%d" % s2), R("aqr%d" % s2)
                sc.dma("aqn%d" % s2, lambda e: e.dma_start(out=qn[s2][:, 0:T], in_=self.QN[hd * 128:(hd + 1) * 128, t0:t0 + T]), writes=[rqn])
                sc.dma("aqr%d" % s2, lambda e: e.dma_start(out=qr[s2][0:64, 0:T], in_=self.QR[hd * 64:(hd + 1) * 64, t0:t0 + T]), writes=[rqr])
                po, rpo = ps[4 + s2], R("psb%d" % (4 + s2))
                pd, rpd = ps[6 + s2], R("psb%d" % (6 + s2))
                n = len(kbs)

                def qk(idx):
                    kb = kbs[idx]
                    b = idx % 4
                    sc.op("pe", lambda e: e.matmul(ps[b][:, 0:T], KNh[:, kb * 128:(kb + 1) * 128], qn[s2][:, 0:T], start=True, stop=False),
                          reads=[R("KNh"), rqn], writes=[R("psb%d" % b)])
                    sc.op("pe", lambda e: e.matmul(ps[b][:, 0:T], KRt[:, kb * 128:(kb + 1) * 128], qr[s2][:, 0:T], start=False, stop=True),
                          reads=[R("KRt"), rqr], writes=[R("psb%d" % b)])

                def rest(idx):
                    kb = kbs[idx]
                    b = idx % 4
                    sc.op("act", lambda e: e.activation(out=pT[b][:, 0:T], in_=ps[b][:, 0:T], func=AF.Exp, bias=self.negB[:], scale=RKa[:, kb, hd:hd + 1]),
                          reads=[R("psb%d" % b), R("negB"), R("RKa")], writes=[R("pT%d" % b)])
                    if self.debug and i == 0:
                        sc.dma("dbgp", lambda e: e.dma_start(out=self.DBG_P[kb * 128:(kb + 1) * 128, 0:T], in_=pT[b][:, 0:T]), reads=[R("pT%d" % b)])
                    sc.op("pe", lambda e: e.matmul(po[:, 0:T], Vh[:, kb, :], pT[b][:, 0:T], start=(idx == 0), stop=(idx == n - 1)),
                          reads=[R("Vh"), R("pT%d" % b)], writes=[rpo])
                    sc.op("pe", lambda e: e.matmul(pd[:, 0:T], self.ones_b[:], pT[b][:, 0:T], start=(idx == 0), stop=(idx == n - 1)),
                          reads=[R("ones_b"), R("pT%d" % b)], writes=[rpd])
                qk(0)
                if n > 1:
                    qk(1)
                for idx in range(n):
                    if idx + 2 < n:
                        qk(idx + 2)
                    rest(idx)
                sc.op("dve", lambda e: e.reciprocal(out=rden[s2][:, 0:T], in_=pd[:, 0:T]), reads=[rpd], writes=[R("rden%d" % s2)])
                if self.debug and i == 0:
                    dbo = self.dbo
                    sc.op("dve", lambda e: e.tensor_copy(out=dbo[:, 0:T], in_=po[:, 0:T]), reads=[rpo], writes=[R("dbo")])
                    sc.dma("dbgo", lambda e: e.dma_start(out=self.DBG_O[:, 0:T], in_=dbo[:, 0:T]), reads=[R("dbo")])
                    sc.dma("dbgd", lambda e: e.dma_start(out=self.DBG_D[:, 0:T], in_=rden[s2][:, 0:T]), reads=[R("rden%d" % s2)])
                sc.op("dve", lambda e: e.tensor_tensor(out=ot[s2][:, 0:T], in0=po[:, 0:T], in1=rden[s2][:, 0:T], op=ALU.mult),
                      reads=[rpo, R("rden%d" % s2)], writes=[R("ot%d" % s2)])
                sc.dma("ots%d" % s2, lambda e: e.dma_start(out=self.AT[hd * 128:(hd + 1) * 128, t0:t0 + T], in_=ot[s2][:, 0:T]), reads=[R("ot%d" % s2)])

            for hd in range(NH):
                for c0 in range(0, ST, CH):
                    cl = min(CH, ST - c0)
                    sc.dma("knl", lambda e, c0=c0, cl=cl, hd=hd: e.dma_start(out=KNh[:, c0:c0 + cl], in_=self.KN[hd * 128:(hd + 1) * 128, c0:c0 + cl]), writes=[R("KNh")])
                vv = self.V.rearrange("(b p) (h d) -> p b h d", p=128, d=128)
                for b0 in range(0, NBT, 16):
                    bl = min(16, NBT - b0)
                    sc.dma("vl", lambda e, b0=b0, bl=bl, hd=hd: e.dma_start(out=Vh[:, b0:b0 + bl, :], in_=vv[:, b0:b0 + bl, hd, :]), writes=[R("Vh")])
                for t0 in range(0, S, 512):
                    qtile(hd, t0, 512, list(range(NBT)))
                if not last:
                    qtile(hd, S, CTX, list(range(NBL, NBT)))
            sc.barrier()
            sc.flush()

    def phase_fourier(self, l, last):
        sc, R = self.sc, self.R
        S = self.S
        cases = [(S, self.L1, self.t1a, self.t1b, self.er, self.ei, 0)]
        if not last:
            cases.append((CTX, 2, self.t1ac, self.t1bc, self.erc, self.eic, S))
        for (Lc, L1, t1a, t1b, er, ei, tok0) in cases:
            with ExitStack() as st:
                ps = self.psum_banks(st, 8)
                stg = [self.sb(st, "stg%d" % i, [128, 2048], F32) for i in range(2)]
                Er = self.sb(st, "Er", [128, Lc], BF16)
                Ei = self.sb(st, "Ei", [128, Lc], BF16)
                Ta = self.sb(st, "Ta", [128, 2 * L1], BF16)
                Tb = self.sb(st, "Tb", [128, 2 * L1], BF16)
                self.load_cast(stg, Er, R("Er"), er, Lc, "er")
                self.load_cast(stg, Ei, R("Ei"), ei, Lc, "ei")
                self.load_cast([s_[0:L1, :] for s_ in stg], Ta[0:L1, :], R("Ta"), t1a, 2 * L1, "ta")
                self.load_cast([s_[0:L1, :] for s_ in stg], Tb[0:L1, :], R("Tb"), t1b, 2 * L1, "tb")
                Dt = [self.sb(st, "Dt%d" % i, [128, 128, 128], BF16) for i in range(1)]
                YT = self.sb(st, "YT", [128, 64, 2 * L1], BF16)
                MTb = self.sb(st, "MTb", [64, Lc], BF16)
                Er3 = Er[:].rearrange("p (k2 k1) -> p k1 k2", k1=L1)
                Ei3 = Ei[:].rearrange("p (k2 k1) -> p k1 k2", k1=L1)
                MT3 = MTb[:].rearrange("p (k2 k1) -> p k1 k2", k1=L1)
                nb = [0]

                def bank():
                    b = nb[0] % 8
                    nb[0] += 1
                    return ps[b], R("psb%d" % b)
                npb = 512 // (2 * L1) if 2 * L1 <= 512 else 1
                npb = min(npb, 64)
                kpb = min(4, L1)

                def block(g, cb):
                    Dd = Dt[0]
                    for part in range(2):
                        row0 = g * 512 + part * 256 + cb * 64
                        for r0 in range(0, 64, 16):
                            sc.dma("dtl", lambda e, part=part, r0=r0, row0=row0: e.dma_start(
                                out=Dd[0:L1, part * 64 + r0:part * 64 + r0 + 16, :],
                                in_=self.GT[row0 + r0:row0 + r0 + 16, tok0:tok0 + Lc].rearrange("r (a b) -> a r b", b=128)), writes=[R("Dt")])
                    for d0 in range(0, 64, npb):
                        pb, rpb = bank()
                        for dd in range(npb):
                            d = d0 + dd
                            o = dd * 2 * L1
                            sc.op("pe", lambda e, d=d, o=o, pb=pb: e.matmul(pb[:, o:o + 2 * L1], Dd[0:L1, d, :], Ta[0:L1, :], start=True, stop=False),
                                  reads=[R("Dt"), R("Ta")], writes=[rpb])
                            sc.op("pe", lambda e, d=d, o=o, pb=pb: e.matmul(pb[:, o:o + 2 * L1], Dd[0:L1, 64 + d, :], Tb[0:L1, :], start=False, stop=True),
                                  reads=[R("Dt"), R("Tb")], writes=[rpb])
                        eng = "act" if (d0 // npb) % 2 == 0 else "dve"
                        src = lambda pb=pb: pb[:, 0:npb * 2 * L1].rearrange("p (d x) -> p d x", x=2 * L1)
                        if eng == "act":
                            sc.op("act", lambda e, d0=d0, pb=pb: e.copy(out=YT[:, d0:d0 + npb, :], in_=pb[:, 0:npb * 2 * L1].rearrange("p (d x) -> p d x", x=2 * L1)),
                                  reads=[rpb], writes=[R("YT")])
                        else:
                            sc.op("dve", lambda e, d0=d0, pb=pb: e.tensor_copy(out=YT[:, d0:d0 + npb, :], in_=pb[:, 0:npb * 2 * L1].rearrange("p (d x) -> p d x", x=2 * L1)),
                                  reads=[rpb], writes=[R("YT")])
                    for k0 in range(0, L1, kpb):
                        pb, rpb = bank()
                        for kk in range(kpb):
                            k1 = k0 + kk
                            sc.op("pe", lambda e, k1=k1, kk=kk, pb=pb: e.matmul(pb[0:64, kk * 128:(kk + 1) * 128], YT[:, :, k1], Er3[:, k1, :], start=True, stop=False),
                                  reads=[R("YT"), R("Er")], writes=[rpb])
                            sc.op("pe", lambda e, k1=k1, kk=kk, pb=pb: e.matmul(pb[0:64, kk * 128:(kk + 1) * 128], YT[:, :, L1 + k1], Ei3[:, k1, :], start=False, stop=True),
                                  reads=[R("YT"), R("Ei")], writes=[rpb])
                        eng = "act" if (k0 // kpb) % 2 == 0 else "dve"
                        if eng == "act":
                            sc.op("act", lambda e, k0=k0, pb=pb: e.copy(out=MT3[:, k0:k0 + kpb, :], in_=pb[0:64, 0:kpb * 128].rearrange("p (q k) -> p q k", k=128)),
                                  reads=[rpb], writes=[R("MTb")])
                        else:
                            sc.op("dve", lambda e, k0=k0, pb=pb: e.tensor_copy(out=MT3[:, k0:k0 + kpb, :], in_=pb[0:64, 0:kpb * 128].rearrange("p (q k) -> p q k", k=128)),
                                  reads=[rpb], writes=[R("MTb")])
                    for c0 in range(0, Lc, 4096):
                        cl = min(4096, Lc - c0)
                        sc.dma("mts", lambda e, c0=c0, cl=cl: e.dma_start(out=self.MT[g * 256 + cb * 64:g * 256 + (cb + 1) * 64, tok0 + c0:tok0 + c0 + cl],
                                                                        in_=MTb[:, c0:c0 + cl]), reads=[R("MTb")])
                for g in range(4):
                    for cb in range(4):
                        block(g, cb)
                sc.barrier()
                sc.flush()

    def phase_mixD(self, l, last):
        sc, R = self.sc, self.R
        tl = self.tiles(with_ctx=not last)
        xv = self.xr.rearrange("(c p) s -> p c s", p=128)
        with ExitStack() as st:
            ps = self.psum_banks(st, 8)
            stg = [self.sb(st, "stg%d" % i, [128, 2048], F32) for i in range(2)]
            Wo = self.sb(st, "Wo", [128, KC * 2048], BF16)
            Wf = self.sb(st, "Wf", [128, 2048], BF16)
            self.load_cast(stg, Wo, R("Wo"), self.wo[l * 128:(l + 1) * 128, :], KC * 2048, "wo")
            self.load_cast(stg, Wf, R("Wf"), self.wf[l * 128:(l + 1) * 128, :], 2048, "wf")
            Wo3 = Wo[:].rearrange("p (c n) -> p c n", c=KC)
            Wf4 = Wf[:].rearrange("p (g c e) -> p g c e", g=4, c=2)
            X = [self.sb(st, "X%d" % i, [128, KC, 512], F32) for i in range(2)]
            A_ = [self.sb(st, "A_%d" % i, [128, KC, 512], BF16) for i in range(1)]
            M_ = [self.sb(st, "M_%d" % i, [128, 8, 512], BF16) for i in range(1)]
            atv = self.AT.rearrange("(c p) s -> p c s", p=128)
            mtv = self.MT.rearrange("(c p) s -> p c s", p=128)
            nb = [0]

            def bank():
                b = nb[0] % 8
                nb[0] += 1
                return ps[b], R("psb%d" % b)

            def tile_body(i, t0, T, col):
                s2 = i % 2
                Xi, Ai, Mi = X[s2], A_[0], M_[0]
                rX, rA, rM = R("X%d" % s2), R("A_0"), R("M_0")
                sc.dma("xl%d" % s2, lambda e: e.dma_start(out=Xi[:, :, 0:T], in_=xv[:, :, t0:t0 + T]), writes=[rX])
                sc.dma("al%d" % s2, lambda e: e.dma_start(out=Ai[:, 0:8, 0:T], in_=atv[:, :, t0:t0 + T]), writes=[rA])
                sc.dma("ml%d" % s2, lambda e: e.dma_start(out=Mi[:, :, 0:T], in_=mtv[:, :, t0:t0 + T]), writes=[rM])
                for g in range(4):
                    for ec in range(2):
                        pb, rpb = bank()
                        for c in range(2):
                            sc.op("pe", lambda e, g=g, ec=ec, c=c, pb=pb: e.matmul(pb[:, 0:T], Wf4[:, g, c, ec * 128:(ec + 1) * 128], Mi[:, 2 * g + c, 0:T],
                                                                                 start=(c == 0), stop=(c == 1)), reads=[R("Wf"), rM], writes=[rpb])
                        sc.op("act", lambda e, g=g, ec=ec, pb=pb: e.copy(out=Ai[:, 8 + 2 * g + ec, 0:T], in_=pb[:, 0:T]), reads=[rpb], writes=[rA])
                for j in range(KC):
                    pb, rpb = bank()
                    for c in range(KC):
                        sc.op("pe", lambda e, j=j, c=c, pb=pb: e.matmul(pb[:, 0:T], Wo3[:, c, j * 128:(j + 1) * 128], Ai[:, c, 0:T],
                                                                      start=(c == 0), stop=(c == KC - 1)), reads=[R("Wo"), rA], writes=[rpb])
                    sc.op("dve", lambda e, j=j, pb=pb: e.scalar_tensor_tensor(
                        out=Xi[:, j, 0:T], in0=pb[:, 0:T], scalar=self.Gat[:, 1, j, col:col + 1], in1=Xi[:, j, 0:T],
                        op0=ALU.mult, op1=ALU.add), reads=[rpb, R("Gat"), rX], writes=[rX])
                sc.dma("xs%d" % s2, lambda e: e.dma_start(out=xv[:, :, t0:t0 + T], in_=Xi[:, :, 0:T]), reads=[rX])
            for i, (t0, T, col) in enumerate(tl):
                tile_body(i, t0, T, col)
            sc.barrier()
            sc.flush()


def _tables(S):
    ST = S + CTX
    f32 = np.float32
    t = np.arange(S)
    row = (t // 64).astype(f32)
    colp = (t % 64).astype(f32)
    inv = (np.float32(10000.0) ** (-np.arange(16, dtype=f32) / np.float32(16))).astype(f32)
    ang = [row[None, :] * inv[:, None], colp[None, :] * inv[:, None]]
    rc = np.ones((64, ST), f32)
    rs = np.zeros((64, ST), f32)
    for p in range(64):
        half, pos, i = p // 32, (p % 32) // 16, p % 16
        rc[p, :S] = np.cos(ang[half][i]).astype(f32)
        sn = np.sin(ang[half][i]).astype(f32)
        rs[p, :S] = -sn if pos == 0 else sn
    c = np.arange(256)
    a = 2 * np.pi * ((c[:, None] * c[None, :]) % 256) / 256
    fc = np.concatenate([np.cos(a), -np.sin(a)], axis=1).astype(f32)
    fct = np.ascontiguousarray(fc.reshape(2, 128, 512).transpose(1, 0, 2).reshape(128, 1024))

    def dft(L):
        L1 = L // 128
        i1 = np.arange(L1)
        a1 = 2 * np.pi * ((i1[:, None] * i1[None, :]) % L1) / L1
        cr, ci = np.cos(a1), -np.sin(a1)
        t1a = np.concatenate([cr, ci], axis=1).astype(f32)
        t1b = np.concatenate([-ci, cr], axis=1).astype(f32)
        l2 = np.arange(128, dtype=np.int64)
        k = np.arange(L, dtype=np.int64)
        a2 = 2 * np.pi * ((l2[:, None] * k[None, :]) % L) / L
        nrm = 1.0 / np.sqrt(L * 256.0)
        er = (np.cos(a2) * nrm).astype(f32)
        ei = (np.sin(a2) * nrm).astype(f32)
        return t1a, t1b, er, ei
    t1a, t1b, er, ei = dft(S)
    t1ac, t1bc, erc, eic = dft(CTX)
    return dict(ropec=rc, ropes=rs, fct=fct, t1a=t1a, t1b=t1b, er=er, ei=ei, t1ac=t1ac, t1bc=t1bc, erc=erc, eic=eic)


def _prep_shared(inp, L):
    f = np.ascontiguousarray
    out = {}
    out["w_ada"] = inp["w_ada"][:L].reshape(L * D, 18432)
    out["b_adaT"] = f(inp["b_ada"][:L].reshape(L, 144, 128).transpose(0, 2, 1)).reshape(L * 128, 144)
    out["norm_gT"] = f(inp["norm_g"][:L].reshape(L, 48, 128).transpose(0, 2, 1)).reshape(L * 128, 48)
    wgu = np.empty((L, 2, HC, 128, 2, KC, 128), np.float32)
    wd = np.empty((L, 2, KC, 128, HC, 128), np.float32)
    ffn = ((inp["ffn1_w_gate"], inp["ffn1_w_up"], inp["ffn1_w_down"]),
           (inp["ffn2_w_gate"], inp["ffn2_w_up"], inp["ffn2_w_down"]))
    for s in range(2):
        for a in range(2):
            w = ffn[s][a][:L].reshape(L, KC, 128, HC, 128)
            wgu[:, s, :, :, a] = w.transpose(0, 3, 2, 1, 4)
        w = ffn[s][2][:L].reshape(L, HC, 128, KC, 128)
        wd[:, s] = w.transpose(0, 3, 2, 1, 4)
    out["wgu"] = wgu.reshape(L * 2 * HC * 128, 2 * KC * 128)
    out["wd"] = wd.reshape(L * 2 * KC * 128, HC * 128)
    perm = np.array([i + 16 if (i % 32) < 16 else i - 16 for i in range(64)])
    cols = np.concatenate([np.arange(0, 832), 768 + perm, np.arange(832, 1856)])
    w = inp["w_in"][:L][:, :, cols].reshape(L, KC, 128, 15, 128)
    out["win"] = f(w.transpose(0, 3, 2, 1, 4)).reshape(L * 15 * 128, KC * 128)
    hq = np.arange(NH)[:, None] * 192
    cq = np.concatenate([(hq + np.arange(128)[None]).ravel(), (hq + 128 + np.arange(64)[None]).ravel(), (hq + 128 + perm[None]).ravel()])
    w = inp["w_uq"][:L][:, :, cq].reshape(L, 4, 128, 2048)
    out["wuq"] = f(w.transpose(0, 2, 1, 3)).reshape(L * 128, 4 * 2048)
    hk = np.arange(NH)[:, None] * 256
    ck = np.concatenate([(hk + np.arange(128)[None]).ravel(), (hk + 128 + np.arange(128)[None]).ravel()])
    w = inp["w_ukv"][:L][:, :, ck].reshape(L, 2, 128, 2048)
    out["wukv"] = f(w.transpose(0, 2, 1, 3)).reshape(L * 128, 2 * 2048)
    w = inp["w_fourier"][:L].reshape(L, 4, 2, 128, 256)
    out["wf"] = f(w.transpose(0, 3, 1, 2, 4)).reshape(L * 128, 2048)
    w = inp["w_o"][:L].reshape(L, KC, 128, 2048)
    out["wo"] = f(w.transpose(0, 2, 1, 3)).reshape(L * 128, KC * 2048)
    out["gql"] = f(inp["q_lat_g"][:L].reshape(L, 4, 128).transpose(0, 2, 1)).reshape(L * 128, 4)
    out["gkvl"] = f(inp["kv_lat_g"][:L].reshape(L, 2, 128).transpose(0, 2, 1)).reshape(L * 128, 2)
    for nm, key in (("gq", "q_norm_g"), ("gk", "k_norm_g")):
        g = inp[key][:L]
        t = np.ones((L, 128, 3), np.float32)
        t[:, :, 0] = g[:, 0:128]
        t[:, 0:64, 1] = g[:, 128:192]
        t[:, 0:64, 2] = g[:, 128 + perm]
        out[nm] = t.reshape(L * 128, 3)
        out[nm + "b"] = f(np.broadcast_to(g[:, None, :], (L, 128, 192))).reshape(L * 128, 192)
    return {k: f(v, dtype=np.float32) if v.dtype != np.float32 else f(v) for k, v in out.items()}


_CACHE = {}


def kernel(**inputs):
    S, L, B = CFG["S"], CFG["DEPTH"], CFG["B"]
    inp = {k: np.asarray(v) for k, v in inputs.items()}
    key = (S, L)
    if key not in _CACHE:
        _CACHE[key] = Builder(S, L).build()
    nc = _CACHE[key]
    shared = _prep_shared(inp, L)
    shared.update(_tables(S))
    in_maps = []
    for b in range(B):
        m = dict(shared)
        m["xT"] = np.ascontiguousarray(inp["x"][b].T)
        m["cxT"] = np.ascontiguousarray(inp["ctx"][b].T)
        m["cT"] = np.ascontiguousarray(np.stack([inp["c"][b], inp["c_ctx"]], axis=1))
        in_maps.append(m)
    res = run_bass_kernel_spmd(nc, in_maps, core_ids=list(range(B)))
    out = np.stack([np.ascontiguousarray(res.results[b]["yT"].T) for b in range(B)], axis=0)
    return out.astype(np.float32)
```

```python
from contextlib import ExitStack
import numpy as np
import concourse.bass as bass
import concourse.mybir as mybir
from concourse.bass_utils import run_bass_kernel_spmd

F32 = mybir.dt.float32
BF16 = mybir.dt.bfloat16
AF = mybir.ActivationFunctionType
ALU = mybir.AluOpType
AX = mybir.AxisListType

CFG = dict(S=16384, DEPTH=4, B=2)
D = 2048
KC = 16
CTX = 256
DFF = 5632
HC = 44
NH = 8
EPS = 1e-6
SM_SCALE = 192.0 ** -0.5
ENGS = ("pe", "act", "dve", "pool", "sp")


class Res:
    __slots__ = ("w", "r")

    def __init__(self):
        self.w = None
        self.r = []


class Sched:
    def __init__(self, nc, stack):
        self.nc = nc
        self.stack = stack
        self.q = {e: [] for e in ENGS}
        self.seq = {e: 0 for e in ENGS}
        self.known = {e: {} for e in ENGS}
        self.sems = {}
        self.semval = {}
        self.reg = {}
        for e in ENGS:
            self.sems[e] = stack.enter_context(nc.semaphore("s_" + e))
            self.semval[e] = 0

    def dsem(self, key):
        if key not in self.sems:
            self.sems[key] = self.stack.enter_context(self.nc.semaphore("d_" + key))
            self.semval[key] = 0
            self.seq[key] = 0
        return key

    def _waits(self, eng, reads, writes):
        need = {}
        for r in reads:
            t = r.w
            if t is not None and need.get(t[0], 0) < t[1]:
                need[t[0]] = t[1]
        for w in writes:
            t = w.w
            if t is not None and t[0] != eng and need.get(t[0], 0) < t[1]:
                need[t[0]] = t[1]
            for t in w.r:
                if t[0] != eng and need.get(t[0], 0) < t[1]:
                    need[t[0]] = t[1]
        kn = self.known[eng]
        out = []
        for k, v in need.items():
            if k == eng and eng == "pe":
                continue
            if kn.get(k, 0) >= v:
                continue
            kn[k] = v
            self.reg[(k, v)][3] = True
            out.append((k, v))
        return out

    def _push(self, eng, key, fn, reads, writes, signal):
        waits = self._waits(eng, reads, writes)
        self.seq[key] += 1
        tok = (key, self.seq[key])
        ent = [waits, fn, key, signal]
        self.reg[tok] = ent
        self.q[eng].append((ent, tok))
        for r in reads:
            r.r.append(tok)
        for w in writes:
            w.w = tok
            w.r = []
        return tok

    def op(self, eng, fn, reads=(), writes=()):
        return self._push(eng, eng, fn, reads, writes, False)

    def dma(self, key, fn, reads=(), writes=(), eng="sp"):
        self.dsem(key)
        return self._push(eng, key, fn, reads, writes, True)

    def barrier(self):
        keys = list(self.sems)
        for e in ENGS:
            waits = []
            kn = self.known[e]
            for k in keys:
                v = self.seq[k]
                if k == e or v == 0 or kn.get(k, 0) >= v or (k, v) not in self.reg:
                    continue
                kn[k] = v
                self.reg[(k, v)][3] = True
                waits.append((k, v))
            self.seq[e] += 1
            tok = (e, self.seq[e])
            ent = [waits, (lambda en: en.nop()), e, True]
            self.reg[tok] = ent
            self.q[e].append((ent, tok))
        nop1 = {e: self.seq[e] for e in ENGS}
        for e in ENGS:
            waits = [(k, nop1[k]) for k in ENGS if k != e]
            self.seq[e] += 1
            tok = (e, self.seq[e])
            ent = [waits, (lambda en: en.nop()), e, False]
            self.reg[tok] = ent
            self.q[e].append((ent, tok))
        for e in ENGS:
            for k in keys:
                self.known[e][k] = self.seq[k]

    def flush(self):
        nc = self.nc
        val = {}
        for name in ENGS:
            for ent, tok in self.q[name]:
                if ent[3]:
                    k = ent[2]
                    self.semval[k] += 1 if k in ENGS else 16
                    val[tok] = self.semval[k]
        with nc.Block() as block:
            for name, deco in (("pe", block.tensor), ("act", block.scalar), ("dve", block.vector),
                               ("pool", block.gpsimd), ("sp", block.sync)):
                lst = self.q[name]
                if not lst:
                    continue

                def body(en, lst=lst):
                    sems = self.sems
                    for ent, tok in lst:
                        waits, fn, sk, sig = ent
                        for k, v in waits:
                            en.wait_ge(sems[k], val[(k, v)])
                        ins = fn(en)
                        if sig:
                            ins.then_inc(sems[sk], 1 if sk in ENGS else 16)
                deco(body)
        self.q = {e: [] for e in ENGS}
        self.reg = {}


class Builder:
    def __init__(self, S, DEPTH, debug=False):
        self.S = S
        self.DEPTH = DEPTH
        self.ST = S + CTX
        self.L1 = S // 128
        self.debug = debug
        self.nc = bass.Bass("TRN2", target_bir_lowering=False)
        self.res = {}

    def R(self, key):
        r = self.res.get(key)
        if r is None:
            r = self.res[key] = Res()
        return r

    def declare(self):
        nc, S, ST, L, L1 = self.nc, self.S, self.ST, self.DEPTH, self.L1
        di = lambda n, sh, dt=F32: nc.dram_tensor(n, sh, dt, kind="ExternalInput").ap()
        if self.debug:
            ds = lambda n, sh, dt=BF16: nc.dram_tensor(n, sh, dt, kind="ExternalOutput").ap()
        else:
            ds = lambda n, sh, dt=BF16: nc.dram_tensor(n, sh, dt).ap()
        self.xT = di("xT", [D, S])
        self.cxT = di("cxT", [D, CTX])
        self.cT = di("cT", [D, 2])
        self.w_ada = di("w_ada", [L * D, 144 * 128])
        self.b_adaT = di("b_adaT", [L * 128, 144])
        self.norm_gT = di("norm_gT", [L * 128, 48])
        self.wgu = di("wgu", [L * 2 * HC * 128, 2 * KC * 128])
        self.wd = di("wd", [L * 2 * KC * 128, HC * 128])
        self.win = di("win", [L * 15 * 128, KC * 128])
        self.wuq = di("wuq", [L * 128, 4 * 2048])
        self.wukv = di("wukv", [L * 128, 2 * 2048])
        self.wf = di("wf", [L * 128, 4 * 2 * 256])
        self.wo = di("wo", [L * 128, KC * 2048])
        self.gql = di("gql", [L * 128, 4])
        self.gkvl = di("gkvl", [L * 128, 2])
        self.gq = di("gq", [L * 128, 3])
        self.gk = di("gk", [L * 128, 3])
        self.gqb = di("gqb", [L * 128, 192])
        self.gkb = di("gkb", [L * 128, 192])
        self.ropec = di("ropec", [64, ST])
        self.ropes = di("ropes", [64, ST])
        self.fct = di("fct", [128, 2 * 512])
        self.t1a = di("t1a", [L1, 2 * L1])
        self.t1b = di("t1b", [L1, 2 * L1])
        self.er = di("er", [128, S])
        self.ei = di("ei", [128, S])
        self.t1ac = di("t1ac", [2, 4])
        self.t1bc = di("t1bc", [2, 4])
        self.erc = di("erc", [128, CTX])
        self.eic = di("eic", [128, CTX])
        self.yT = nc.dram_tensor("yT", [D, S], F32, kind="ExternalOutput").ap()
        self.xr = ds("xr", [D, ST], F32)
        self.wgu_b = ds("wgu_b", [2 * HC * 128, 2 * KC * 128])
        self.wd_b = ds("wd_b", [2 * KC * 128, HC * 128])
        self.win_b = ds("win_b", [15 * 128, KC * 128])
        self.QN = ds("QN", [NH * 128, ST])
        self.QR = ds("QR", [NH * 64, ST])
        self.KN = ds("KN", [NH * 128, ST])
        self.KR = ds("KR", [64, ST])
        self.V = ds("V", [ST, 1024])
        self.RK = ds("RK", [ST, 8], F32)
        self.GT = ds("GT", [4 * 512, ST])
        self.AT = ds("AT", [1024, ST])
        self.MT = ds("MT", [1024, ST])
        if self.debug:
            self.DBG_P = ds("DBG_P", [ST, 512])
            self.DBG_B = ds("DBG_B", [128, 4], F32)
            self.DBG_G = ds("DBG_G", [128, 192], F32)
            self.DBG_D = ds("DBG_D", [128, 512], F32)
            self.DBG_O = ds("DBG_O", [128, 512], F32)

    def sb(self, st, name, shape, dt=F32):
        self.uid = getattr(self, "uid", 0) + 1
        return st.enter_context(self.nc.sbuf_tensor("%s_%d" % (name, self.uid), shape, dt))

    def psum_banks(self, st, n=8):
        self.uid = getattr(self, "uid", 0) + 1
        return [st.enter_context(self.nc.psum_tensor("psb%d_%d" % (i, self.uid), [128, 512], F32)) for i in range(n)]

    def build(self):
        nc = self.nc
        self.declare()
        with ExitStack() as gst:
            self.sc = Sched(nc, gst)
            sc = self.sc
            self.ones_f = self.sb(gst, "ones_f", [128, 128], F32)
            self.ones_b = self.sb(gst, "ones_b", [128, 128], BF16)
            self.modT = self.sb(gst, "modT", [128, 144, 2], F32)
            self.Asc = self.sb(gst, "Asc", [128, 3, KC, 2], F32)
            self.Gat = self.sb(gst, "Gat", [128, 3, KC, 2], F32)
            self.gql_s = self.sb(gst, "gql_s", [128, 4], F32)
            self.gkvl_s = self.sb(gst, "gkvl_s", [128, 2], F32)
            self.gq_s = self.sb(gst, "gq_s", [128, 3], F32)
            self.gk_s = self.sb(gst, "gk_s", [128, 3], F32)
            self.negB = self.sb(gst, "negB", [128, 1], F32)
            sc.op("dve", lambda e: e.memset(self.ones_f[:], 1.0), writes=[self.R("ones_f")])
            sc.op("dve", lambda e: e.memset(self.ones_b[:], 1.0), writes=[self.R("ones_b")])
            self.copy_x_in()
            stop = self.debug if isinstance(self.debug, int) and not isinstance(self.debug, bool) else 99
            for l in range(self.DEPTH):
                last = l == self.DEPTH - 1
                self.phase_mod(l)
                self.phase_wconv(l)
                if stop >= 1:
                    self.phase_ffn(l, 0)
                if stop >= 2:
                    self.phase_mixA(l)
                if stop >= 3:
                    self.phase_attn(l, last)
                if stop >= 4:
                    self.phase_fourier(l, last)
                if stop >= 5:
                    self.phase_mixD(l, last)
                if stop >= 6:
                    self.phase_ffn(l, 1, last)
        return nc

    def copy_x_in(self):
        sc, S = self.sc, self.S
        with ExitStack() as st:
            buf = [self.sb(st, "cpb%d" % i, [128, KC, 512], F32) for i in range(2)]
            srcs = [(self.xT, t0, 512, t0) for t0 in range(0, S, 512)] + [(self.cxT, 0, CTX, S)]
            for i, (src, s0, T, d0) in enumerate(srcs):
                b = buf[i % 2]
                rb = self.R("cpb%d" % (i % 2))
                sv = src.rearrange("(c p) s -> p c s", p=128)[:, :, s0:s0 + T]
                dv = self.xr.rearrange("(c p) s -> p c s", p=128)[:, :, d0:d0 + T]
                sc.dma("cpl%d" % (i % 2), lambda e, b=b, sv=sv, T=T: e.dma_start(out=b[:, :, 0:T], in_=sv), writes=[rb])
                sc.dma("cps%d" % (i % 2), lambda e, b=b, dv=dv, T=T: e.dma_start(out=dv, in_=b[:, :, 0:T]), reads=[rb])
            sc.barrier()
            sc.flush()

    def phase_mod(self, l):
        sc, nc = self.sc, self.nc
        R = self.R
        with ExitStack() as st:
            ps = self.psum_banks(st, 2)
            cond = self.sb(st, "cond", [128, KC, 2], F32)
            scd = self.sb(st, "scd", [128, KC, 2], F32)
            bt = self.sb(st, "bt", [128, 144], F32)
            ng = self.sb(st, "ng", [128, 48], F32)
            gqb = self.sb(st, "gqb_s", [128, 192], F32)
            gkb = self.sb(st, "gkb_s", [128, 192], F32)
            mq = self.sb(st, "mq", [128, 1], F32)
            mk = self.sb(st, "mk", [128, 1], F32)
            wa = [self.sb(st, "wa%d" % i, [128, KC, 1024], F32) for i in range(2)]
            sc.dma("m_c", lambda e: e.dma_start(out=cond[:], in_=self.cT.rearrange("(c p) n -> p c n", p=128)), writes=[R("cond")])
            sc.dma("m_b", lambda e: e.dma_start(out=bt[:], in_=self.b_adaT[l * 128:(l + 1) * 128, :]), writes=[R("bt")])
            sc.dma("m_g", lambda e: e.dma_start(out=ng[:], in_=self.norm_gT[l * 128:(l + 1) * 128, :]), writes=[R("ng")])
            for nm, dst, src in (("gql", self.gql_s, self.gql), ("gkvl", self.gkvl_s, self.gkvl), ("gq", self.gq_s, self.gq),
                                 ("gk", self.gk_s, self.gk), ("gqb", gqb, self.gqb), ("gkb", gkb, self.gkb)):
                sc.dma("m_" + nm, lambda e, dst=dst, src=src: e.dma_start(out=dst[:], in_=src[l * 128:(l + 1) * 128, :]), writes=[R(nm)])
            sc.op("act", lambda e: e.activation(out=scd[:], in_=cond[:], func=AF.Silu), reads=[R("cond")], writes=[R("scd")])
            wav = self.w_ada[l * D:(l + 1) * D, :].rearrange("(c p) n -> p c n", p=128)
            for blk in range(18):
                w = wa[blk % 2]
                rw = R("wa%d" % (blk % 2))
                for half in range(2):
                    sc.dma("m_w%d" % (blk % 2), lambda e, w=w, blk=blk, half=half: e.dma_start(
                        out=w[:, half * 8:(half + 1) * 8, :], in_=wav[:, half * 8:(half + 1) * 8, blk * 1024:(blk + 1) * 1024]), writes=[rw])
                for jj in range(8):
                    j = blk * 8 + jj
                    for kc in range(KC):
                        sc.op("pe", lambda e, w=w, jj=jj, kc=kc, j=j: e.matmul(
                            ps[0][:, 2 * j:2 * j + 2], w[:, kc, jj * 128:(jj + 1) * 128], scd[:, kc, :],
                            start=(kc == 0), stop=(kc == KC - 1)), reads=[rw, R("scd")], writes=[R("mps")])
            for n in range(2):
                sc.op("dve", lambda e, n=n: e.tensor_tensor(
                    out=self.modT[:, :, n], in0=ps[0][:, 0:288].rearrange("p (j n) -> p j n", n=2)[:, :, n], in1=bt[:], op=ALU.add),
                    reads=[R("mps"), R("bt")], writes=[R("modT")])
            for s in range(3):
                for n in range(2):
                    sc.op("dve", lambda e, s=s, n=n: e.scalar_tensor_tensor(
                        out=self.Asc[:, s, :, n], in0=self.modT[:, (3 * s + 1) * KC:(3 * s + 2) * KC, n], scalar=1.0,
                        in1=ng[:, s * KC:(s + 1) * KC], op0=ALU.add, op1=ALU.mult),
                        reads=[R("modT"), R("ng")], writes=[R("Asc")])
                    sc.op("dve", lambda e, s=s, n=n: e.tensor_scalar(
                        out=self.Gat[:, s, :, n], in0=self.modT[:, (3 * s + 2) * KC:(3 * s + 3) * KC, n],
                        scalar1=(1.0 if s == 1 else 0.5), scalar2=None, op0=ALU.mult),
                        reads=[R("modT")], writes=[R("Gat")])
            sc.op("dve", lambda e: e.tensor_reduce(out=mq[:], in_=gqb[:], axis=AX.X, op=ALU.max, apply_absolute_value=True),
                  reads=[R("gqb")], writes=[R("mq")])
            sc.op("dve", lambda e: e.tensor_reduce(out=mk[:], in_=gkb[:], axis=AX.X, op=ALU.max, apply_absolute_value=True),
                  reads=[R("gkb")], writes=[R("mk")])
            sc.op("dve", lambda e: e.tensor_scalar(out=self.negB[:], in0=mq[:], scalar1=mk[:, 0:1], scalar2=-SM_SCALE * 192.0,
                                                   op0=ALU.mult, op1=ALU.mult),
                  reads=[R("mq"), R("mk")], writes=[R("negB")])
            if self.debug:
                sc.dma("dbgb0", lambda e: e.dma_start(out=self.DBG_B[:, 0:1], in_=mq[:], allow_slow_non_contiguous=True), reads=[R("mq")])
                sc.dma("dbgb1", lambda e: e.dma_start(out=self.DBG_B[:, 1:2], in_=mk[:], allow_slow_non_contiguous=True), reads=[R("mk")])
                sc.dma("dbgb2", lambda e: e.dma_start(out=self.DBG_B[:, 2:3], in_=self.negB[:], allow_slow_non_contiguous=True), reads=[R("negB")])
                sc.dma("dbgb3", lambda e: e.dma_start(out=self.DBG_G[:, :], in_=gqb[:]), reads=[R("gqb")])
            sc.barrier()
            sc.flush()

    def phase_wconv(self, l):
        sc = self.sc
        R = self.R
        with ExitStack() as st:
            NB = 3
            fb = [self.sb(st, "wcf%d" % i, [128, 8192], F32) for i in range(NB)]
            bb = [self.sb(st, "wcb%d" % i, [128, 8192], BF16) for i in range(NB)]
            jobs = []
            def add(src, dst, row0, nblk, F):
                for b in range(nblk):
                    for f0 in range(0, F, 8192):
                        fl = min(8192, F - f0)
                        jobs.append((src[row0 + b * 128:row0 + (b + 1) * 128, f0:f0 + fl], dst[b * 128:(b + 1) * 128, f0:f0 + fl], fl))
            add(self.wgu, self.wgu_b, l * 2 * HC * 128, 2 * HC, 2 * KC * 128)
            add(self.wd, self.wd_b, l * 2 * KC * 128, 2 * KC, HC * 128)
            add(self.win, self.win_b, l * 15 * 128, 15, KC * 128)
            engs = ("dve", "act", "pool")
            for i, (sv, dv, fl) in enumerate(jobs):
                s = i % NB
                rf, rb = R("wcf%d" % s), R("wcb%d" % s)
                sc.dma("wcl%d" % s, lambda e, s=s, sv=sv, fl=fl: e.dma_start(out=fb[s][:, 0:fl], in_=sv), writes=[rf])
                en = engs[i % 3]
                if en == "act":
                    sc.op("act", lambda e, s=s, fl=fl: e.copy(out=bb[s][:, 0:fl], in_=fb[s][:, 0:fl]), reads=[rf], writes=[rb])
                else:
                    sc.op(en, lambda e, s=s, fl=fl: e.tensor_copy(out=bb[s][:, 0:fl], in_=fb[s][:, 0:fl]), reads=[rf], writes=[rb])
                sc.dma("wcs%d" % s, lambda e, s=s, dv=dv, fl=fl: e.dma_start(out=dv, in_=bb[s][:, 0:fl]), reads=[rb])
            sc.barrier()
            sc.flush()

    def emit_norm_h(self, X, rX, h, rh, T, s, col, ps_ss, tmp):
        sc, R = self.sc, self.R
        sq, rstd, t2 = tmp["sq"], tmp["rstd"], tmp["t2"]
        for c in range(KC):
            k = c % 2
            sc.op("dve", lambda e, c=c, k=k: e.tensor_tensor(out=sq[k][:, 0:T], in0=X[:, c, 0:T], in1=X[:, c, 0:T], op=ALU.mult),
                  reads=[rX], writes=[R("sq%d" % k)])
            sc.op("pe", lambda e, c=c, k=k: e.matmul(ps_ss[:, 0:T], self.ones_f[:], sq[k][:, 0:T], start=(c == 0), stop=(c == KC - 1)),
                  reads=[R("sq%d" % k), R("ones_f")], writes=[R("ps_ss")])
        sc.op("act", lambda e: e.activation(out=rstd[:, 0:T], in_=ps_ss[:, 0:T], func=AF.Sqrt, bias=self.eps_t[:], scale=1.0 / D),
              reads=[R("ps_ss"), R("eps_t")], writes=[R("rstd")])
        sc.op("dve", lambda e: e.reciprocal(out=rstd[:, 0:T], in_=rstd[:, 0:T]), reads=[R("rstd")], writes=[R("rstd")])
        for c in range(KC):
            k = c % 2
            sc.op("dve", lambda e, c=c, k=k: e.scalar_tensor_tensor(
                out=t2[k][:, 0:T], in0=X[:, c, 0:T], scalar=self.Asc[:, s, c, col:col + 1], in1=rstd[:, 0:T],
                op0=ALU.mult, op1=ALU.mult), reads=[rX, R("rstd"), R("Asc")], writes=[R("t2%d" % k)])
            sc.op("act", lambda e, c=c, k=k: e.activation(
                out=h[:, c, 0:T], in_=t2[k][:, 0:T], func=AF.Identity, bias=self.modT[:, 3 * s * KC + c, col:col + 1], scale=1.0),
                reads=[R("t2%d" % k), R("modT")], writes=[rh])

    def alloc_norm_tmp(self, st):
        tmp = dict(sq=[self.sb(st, "sq%d" % i, [128, 512], F32) for i in range(2)],
                   t2=[self.sb(st, "t2%d" % i, [128, 512], F32) for i in range(2)],
                   rstd=self.sb(st, "rstd", [128, 512], F32))
        self.eps_t = self.sb(st, "eps_t", [128, 1], F32)
        self.sc.op("dve", lambda e: e.memset(self.eps_t[:], EPS), writes=[self.R("eps_t")])
        return tmp

    def tiles(self, with_ctx=True):
        t = [(t0, 512, 0) for t0 in range(0, self.S, 512)]
        if with_ctx:
            t.append((self.S, CTX, 1))
        return t

    def phase_ffn(self, l, which, last=False):
        sc, R = self.sc, self.R
        s = 0 if which == 0 else 2
        final = last and which == 1
        tl = self.tiles(with_ctx=not final)
        xv = self.xr.rearrange("(c p) s -> p c s", p=128)
        yv = self.yT.rearrange("(c p) s -> p c s", p=128)
        with ExitStack() as st:
            ps = self.psum_banks(st, 8)
            X = [self.sb(st, "X%d" % i, [128, KC, 512], F32) for i in range(2)]
            h = self.sb(st, "h", [128, KC, 512], BF16)
            act = self.sb(st, "actb", [128, HC, 512], BF16)
            NG = 3
            gub = [self.sb(st, "gub%d" % i, [128, 2, KC, 128], BF16) for i in range(NG)]
            db = [self.sb(st, "db%d" % i, [128, HC, 128], BF16) for i in range(2)]
            sg = [self.sb(st, "sg%d" % i, [128, 512], F32) for i in range(2)]
            tmp = self.alloc_norm_tmp(st)
            ps_ss, ps_g, ps_u, ps_o = ps[0], ps[1:3], ps[3:5], ps[5:7]
            guv = self.wgu_b[which * HC * 128:(which + 1) * HC * 128, :]
            dv_ = self.wd_b[which * KC * 128:(which + 1) * KC * 128, :]

            def load_x(i):
                t0, T, col = tl[i]
                sc.dma("xl%d" % (i % 2), lambda e: e.dma_start(out=X[i % 2][:, :, 0:T], in_=xv[:, :, t0:t0 + T]), writes=[R("X%d" % (i % 2))])

            def norm(i):
                t0, T, col = tl[i]
                self.emit_norm_h(X[i % 2], R("X%d" % (i % 2)), h, R("h"), T, s, col, ps_ss, tmp)

            load_x(0)
            norm(0)
            gi = 0
            di_ = 0
            def tile_body(i, t0, T, col):
                nonlocal gi, di_
                Xi, rX = X[i % 2], R("X%d" % (i % 2))
                if i + 1 < len(tl):
                    load_x(i + 1)
                for j in range(HC):
                    g = gi % NG
                    gi += 1
                    rg = R("gub%d" % g)
                    sc.dma("gul%d" % g, lambda e, g=g, j=j: e.dma_start(
                        out=gub[g][:], in_=guv[j * 128:(j + 1) * 128, :].rearrange("p (a c m) -> p a c m", a=2, c=KC)), writes=[rg])
                    k = j % 2
                    for a, pst, nm in ((0, ps_g[k], "psg%d" % k), (1, ps_u[k], "psu%d" % k)):
                        for c in range(KC):
                            sc.op("pe", lambda e, g=g, a=a, c=c, pst=pst: e.matmul(
                                pst[:, 0:T], gub[g][:, a, c, :], h[:, c, 0:T], start=(c == 0), stop=(c == KC - 1)),
                                reads=[rg, R("h")], writes=[R(nm)])
                    sc.op("act", lambda e, k=k: e.activation(out=sg[k][:, 0:T], in_=ps_g[k][:, 0:T], func=AF.Silu),
                          reads=[R("psg%d" % k)], writes=[R("sg%d" % k)])
                    sc.op("dve", lambda e, k=k, j=j: e.tensor_tensor(out=act[:, j, 0:T], in0=sg[k][:, 0:T], in1=ps_u[k][:, 0:T], op=ALU.mult),
                          reads=[R("sg%d" % k), R("psu%d" % k)], writes=[R("act")])
                if i + 1 < len(tl):
                    norm(i + 1)
                for j in range(KC):
                    d = di_ % 2
                    di_ += 1
                    rd = R("db%d" % d)
                    sc.dma("dl%d" % d, lambda e, d=d, j=j: e.dma_start(
                        out=db[d][:], in_=dv_[j * 128:(j + 1) * 128, :].rearrange("p (c m) -> p c m", c=HC)), writes=[rd])
                    k = j % 2
                    for c in range(HC):
                        sc.op("pe", lambda e, d=d, c=c, k=k: e.matmul(
                            ps_o[k][:, 0:T], db[d][:, c, :], act[:, c, 0:T], start=(c == 0), stop=(c == HC - 1)),
                            reads=[rd, R("act")], writes=[R("pso%d" % k)])
                    sc.op("dve", lambda e, j=j, k=k, Xi=Xi: e.scalar_tensor_tensor(
                        out=Xi[:, j, 0:T], in0=ps_o[k][:, 0:T], scalar=self.Gat[:, s, j, col:col + 1], in1=Xi[:, j, 0:T],
                        op0=ALU.mult, op1=ALU.add), reads=[R("pso%d" % k), R("Gat"), rX], writes=[rX])
                dst = yv if final else xv
                sc.dma("xs%d" % (i % 2), lambda e, Xi=Xi, dst=dst: e.dma_start(out=dst[:, :, t0:t0 + T], in_=Xi[:, :, 0:T]), reads=[rX])
            for i, (t0, T, col) in enumerate(tl):
                tile_body(i, t0, T, col)
            sc.barrier()
            sc.flush()

    def load_cast(self, st_tmp, dst, dst_res, src, F, tag):
        sc, R = self.sc, self.R
        stg = st_tmp
        engs = ("dve", "pool")
        CHK = stg[0].shape[-1]
        for i, f0 in enumerate(range(0, F, CHK)):
            fl = min(CHK, F - f0)
            k = i % 2
            rs = R("lcs%d" % k)
            sc.dma("lc%d" % k, lambda e, k=k, f0=f0, fl=fl: e.dma_start(out=stg[k][:, 0:fl], in_=src[:, f0:f0 + fl]), writes=[rs])
            sc.op(engs[i % 2], lambda e, k=k, f0=f0, fl=fl: e.tensor_copy(out=dst[:, f0:f0 + fl], in_=stg[k][:, 0:fl]),
                  reads=[rs], writes=[dst_res])

    def phase_mixA(self, l):
        sc, R = self.sc, self.R
        S, ST = self.S, self.ST
        tl = [(t0, 256, 0) for t0 in range(0, S, 256)] + [(S, CTX, 1)]
        xv = self.xr.rearrange("(c p) s -> p c s", p=128)
        with ExitStack() as st:
            ps = self.psum_banks(st, 8)
            stg = [self.sb(st, "stg%d" % i, [128, 2048], F32) for i in range(2)]
            Wuq = self.sb(st, "Wuq", [128, 4 * 2048], BF16)
            Wukv = self.sb(st, "Wukv", [128, 2 * 2048], BF16)
            FC = self.sb(st, "FC", [128, 2 * 512], BF16)
            self.load_cast(stg, Wuq, R("Wuq"), self.wuq[l * 128:(l + 1) * 128, :], 4 * 2048, "uq")
            self.load_cast(stg, Wukv, R("Wukv"), self.wukv[l * 128:(l + 1) * 128, :], 2 * 2048, "ukv")
            self.load_cast(stg, FC, R("FC"), self.fct, 1024, "fc")
            Wuq3 = Wuq[:].rearrange("p (c n) -> p c n", c=4)
            Wukv3 = Wukv[:].rearrange("p (c n) -> p c n", c=2)
            FC3 = FC[:].rearrange("p (c n) -> p c n", c=2)
            X = self.sb(st, "X", [128, KC, 256], F32)
            h = self.sb(st, "h", [128, KC, 256], BF16)
            wsl = [self.sb(st, "wsl%d" % i, [128, KC, 128], BF16) for i in range(3)]
            ql = self.sb(st, "ql", [128, 4, 256], F32)
            qn = self.sb(st, "qn", [128, 4, 256], BF16)
            kvl = self.sb(st, "kvl", [128, 2, 256], F32)
            kvn = self.sb(st, "kvn", [128, 2, 256], BF16)
            krr = self.sb(st, "krr", [64, 2, 256], F32)
            fT = self.sb(st, "fT", [128, 8, 256], BF16)
            QNs = self.sb(st, "QNs", [128, NH, 256], BF16)
            QRs = self.sb(st, "QRs", [64, NH, 256], BF16)
            KNs = self.sb(st, "KNs", [128, NH, 256], BF16)
            KRs = self.sb(st, "KRs", [64, 256], BF16)
            Vs = self.sb(st, "Vs", [128, 4, 1024], BF16)
            RKs = self.sb(st, "RKs", [128, 4, 8], F32)
            GTs = self.sb(st, "GTs", [128, 16, 256], BF16)
            rc = self.sb(st, "rc", [64, 256], F32)
            rs_ = self.sb(st, "rs", [64, 256], F32)
            sqk = [self.sb(st, "sqk%d" % i, [128, 256], F32) for i in range(2)]
            sqr = [self.sb(st, "sqr%d" % i, [64, 256], F32) for i in range(2)]
            rq = [self.sb(st, "rq%d" % i, [128, 256], F32) for i in range(2)]
            ra = [self.sb(st, "ra%d" % i, [64, 256], F32) for i in range(2)]
            rb = [self.sb(st, "rb%d" % i, [64, 256], F32) for i in range(2)]
            e192 = self.sb(st, "e192", [128, 1], F32)
            sqkb = [self.sb(st, "sqkb%d" % i, [128, 256], BF16) for i in range(2)]
            sqrb = self.sb(st, "sqrb", [128, 256], BF16)
            sc.op("pool", lambda e: e.memset(sqrb[:], 0.0), writes=[R("sqrb")])
            tmp = self.alloc_norm_tmp(st)
            sc.op("dve", lambda e: e.memset(e192[:], EPS), writes=[R("e192")])
            nps = [0]

            def bank():
                b = nps[0] % 6 + 2
                nps[0] += 1
                return ps[b], R("psb%d" % b)
            ps_ss = ps[0]
            ps_rk, r_rk = ps[1], R("psb1")
            wi = [0]
            winv = self.win_b

            def load_slab(j):
                k = wi[0] % 3
                wi[0] += 1
                sc.dma("wsl%d" % k, lambda e: e.dma_start(
                    out=wsl[k][:], in_=winv[j * 128:(j + 1) * 128, :].rearrange("p (c m) -> p c m", c=KC)), writes=[R("wsl%d" % k)])
                return wsl[k], R("wsl%d" % k)

            def tile_body(i, t0, T, col):
                nb = T // 128
                sc.dma("xl", lambda e: e.dma_start(out=X[:, :, 0:T], in_=xv[:, :, t0:t0 + T]), writes=[R("X")])
                sc.dma("rcl", lambda e: e.dma_start(out=rc[:, 0:T], in_=self.ropec[:, t0:t0 + T]), writes=[R("rc")])
                sc.dma("rsl", lambda e: e.dma_start(out=rs_[:, 0:T], in_=self.ropes[:, t0:t0 + T]), writes=[R("rs")])
                self.emit_norm_h(X, R("X"), h, R("h"), T, 1, col, ps_ss, tmp)
                for j in range(15):
                    w, rw = load_slab(j)
                    if j == 6:
                        for half in range(2):
                            pb, rpb = bank()
                            for c in range(KC):
                                sc.op("pe", lambda e, w=w, c=c, pb=pb, half=half: e.matmul(
                                    pb[0:64, 0:T], w[:, c, half * 64:(half + 1) * 64], h[:, c, 0:T], start=(c == 0), stop=(c == KC - 1)),
                                    reads=[rw, R("h")], writes=[rpb])
                            sc.op("act", lambda e, pb=pb, half=half: e.copy(out=krr[:, half, 0:T], in_=pb[0:64, 0:T]), reads=[rpb], writes=[R("krr")])
                        continue
                    pb, rpb = bank()
                    for c in range(KC):
                        sc.op("pe", lambda e, w=w, c=c, pb=pb: e.matmul(pb[:, 0:T], w[:, c, :], h[:, c, 0:T], start=(c == 0), stop=(c == KC - 1)),
                              reads=[rw, R("h")], writes=[rpb])
                    if j < 4:
                        sc.op("act", lambda e, pb=pb, j=j: e.copy(out=ql[:, j, 0:T], in_=pb[:, 0:T]), reads=[rpb], writes=[R("ql")])
                    elif j < 6:
                        sc.op("act", lambda e, pb=pb, j=j: e.copy(out=kvl[:, j - 4, 0:T], in_=pb[:, 0:T]), reads=[rpb], writes=[R("kvl")])
                    else:
                        sc.op("act", lambda e, pb=pb, j=j: e.copy(out=fT[:, j - 7, 0:T], in_=pb[:, 0:T]), reads=[rpb], writes=[R("fT")])
                for (src, rsrc, nchunk, dstn, rdst, gain, rg) in ((ql, R("ql"), 4, qn, R("qn"), self.gql_s, R("gql")),
                                                               (kvl, R("kvl"), 2, kvn, R("kvn"), self.gkvl_s, R("gkvl"))):
                    pb, rpb = bank()
                    for c in range(nchunk):
                        k = c % 2
                        sc.op("pool", lambda e, src=src, c=c, k=k: e.tensor_tensor(out=sqk[k][:, 0:T], in0=src[:, c, 0:T], in1=src[:, c, 0:T], op=ALU.mult),
                              reads=[rsrc], writes=[R("sqk%d" % k)])
                        sc.op("pe", lambda e, c=c, k=k, pb=pb, nchunk=nchunk: e.matmul(pb[:, 0:T], self.ones_f[:], sqk[k][:, 0:T], start=(c == 0), stop=(c == nchunk - 1)),
                              reads=[R("sqk%d" % k), R("ones_f")], writes=[rpb])
                    sc.op("act", lambda e, pb=pb, nchunk=nchunk: e.activation(out=rq[0][:, 0:T], in_=pb[:, 0:T], func=AF.Sqrt, bias=e192[:], scale=1.0 / (128 * nchunk)),
                          reads=[rpb, R("e192")], writes=[R("rq0")])
                    sc.op("dve", lambda e: e.reciprocal(out=rq[0][:, 0:T], in_=rq[0][:, 0:T]), reads=[R("rq0")], writes=[R("rq0")])
                    for c in range(nchunk):
                        sc.op("dve", lambda e, src=src, c=c, dstn=dstn, gain=gain: e.scalar_tensor_tensor(
                            out=dstn[:, c, 0:T], in0=src[:, c, 0:T], scalar=gain[:, c:c + 1], in1=rq[0][:, 0:T], op0=ALU.mult, op1=ALU.mult),
                            reads=[rsrc, R("rq0"), rg], writes=[rdst])
                sc.op("dve", lambda e: e.scalar_tensor_tensor(out=ra[0][:, 0:T], in0=krr[:, 0, 0:T], scalar=self.gk_s[0:64, 1:2], in1=rc[:, 0:T],
                                                              op0=ALU.mult, op1=ALU.mult), reads=[R("krr"), R("gk"), R("rc")], writes=[R("ra0")])
                sc.op("dve", lambda e: e.scalar_tensor_tensor(out=rb[0][:, 0:T], in0=krr[:, 1, 0:T], scalar=self.gk_s[0:64, 2:3], in1=rs_[:, 0:T],
                                                              op0=ALU.mult, op1=ALU.mult), reads=[R("krr"), R("gk"), R("rs")], writes=[R("rb0")])
                sc.op("dve", lambda e: e.tensor_tensor(out=KRs[:, 0:T], in0=ra[0][:, 0:T], in1=rb[0][:, 0:T], op=ALU.add),
                      reads=[R("ra0"), R("rb0")], writes=[R("KRs")])
                sc.dma("krs", lambda e: e.dma_start(out=self.KR[:, t0:t0 + T], in_=KRs[:, 0:T]), reads=[R("KRs")])
                sc.op("pool", lambda e: e.tensor_tensor(out=sqrb[0:64, 0:T], in0=krr[:, 0, 0:T], in1=krr[:, 0, 0:T], op=ALU.mult),
                      reads=[R("krr")], writes=[R("sqrb")])
                for hd in range(NH):
                    k = hd % 2
                    pb, rpb = bank()
                    for c in range(2):
                        sc.op("pe", lambda e, c=c, pb=pb, hd=hd: e.matmul(pb[:, 0:T], Wukv3[:, c, hd * 128:(hd + 1) * 128], kvn[:, c, 0:T],
                                                                        start=(c == 0), stop=(c == 1)), reads=[R("Wukv"), R("kvn")], writes=[rpb])
                    sc.op("act", lambda e, pb=pb, hd=hd: e.activation(out=KNs[:, hd, 0:T], in_=pb[:, 0:T], func=AF.Identity, scale=self.gk_s[:, 0:1]),
                          reads=[rpb, R("gk")], writes=[R("KNs")])
                    sc.op("act", lambda e, pb=pb, k=k: e.activation(out=sqkb[k][:, 0:T], in_=pb[:, 0:T], func=AF.Square),
                          reads=[rpb], writes=[R("sqkb%d" % k)])
                    for b in range(nb):
                        o2 = (b * 8 + hd) * 2
                        sc.op("pe", lambda e, b=b, k=k, o2=o2: e.matmul(ps_rk[:, o2:o2 + 2], sqkb[k][:, b * 128:(b + 1) * 128], self.ones_b[:, 0:2],
                                                                       start=True, stop=False), reads=[R("sqkb%d" % k), R("ones_b")], writes=[r_rk])
                        sc.op("pe", lambda e, b=b, o2=o2: e.matmul(ps_rk[:, o2:o2 + 2], sqrb[:, b * 128:(b + 1) * 128], self.ones_b[:, 0:2],
                                                                  start=False, stop=True), reads=[R("sqrb"), R("ones_b")], writes=[r_rk])
                    pn, rpn = bank()
                    for c in range(4):
                        sc.op("pe", lambda e, c=c, pn=pn, hd=hd: e.matmul(pn[:, 0:T], Wuq3[:, c, hd * 128:(hd + 1) * 128], qn[:, c, 0:T],
                                                                        start=(c == 0), stop=(c == 3)), reads=[R("Wuq"), R("qn")], writes=[rpn])
                    pr, rpr = bank()
                    for c in range(4):
                        sc.op("pe", lambda e, c=c, pr=pr, hd=hd: e.matmul(pr[0:64, 0:T], Wuq3[:, c, 1024 + hd * 64:1024 + (hd + 1) * 64], qn[:, c, 0:T],
                                                                        start=(c == 0), stop=(c == 3)), reads=[R("Wuq"), R("qn")], writes=[rpr])
                    pw, rpw = bank()
                    for c in range(4):
                        sc.op("pe", lambda e, c=c, pw=pw, hd=hd: e.matmul(pw[0:64, 0:T], Wuq3[:, c, 1536 + hd * 64:1536 + (hd + 1) * 64], qn[:, c, 0:T],
                                                                        start=(c == 0), stop=(c == 3)), reads=[R("Wuq"), R("qn")], writes=[rpw])
                    k2 = 1 - k
                    sc.op("act", lambda e, pn=pn, k2=k2: e.activation(out=sqk[k2][:, 0:T], in_=pn[:, 0:T], func=AF.Square),
                          reads=[rpn], writes=[R("sqk%d" % k2)])
                    sc.op("act", lambda e, pr=pr: e.activation(out=sqr[1][:, 0:T], in_=pr[0:64, 0:T], func=AF.Square),
                          reads=[rpr], writes=[R("sqr1")])
                    pq, rpq = bank()
                    sc.op("pe", lambda e, pq=pq, k2=k2: e.matmul(pq[:, 0:T], self.ones_f[:], sqk[k2][:, 0:T], start=True, stop=False),
                          reads=[R("sqk%d" % k2), R("ones_f")], writes=[rpq])
                    sc.op("pe", lambda e, pq=pq: e.matmul(pq[:, 0:T], self.ones_f[0:64, :], sqr[1][:, 0:T], start=False, stop=True),
                          reads=[R("sqr1"), R("ones_f")], writes=[rpq])
                    sc.op("act", lambda e, pq=pq: e.activation(out=rq[1][:, 0:T], in_=pq[:, 0:T], func=AF.Sqrt, bias=e192[:], scale=1.0 / 192.0),
                          reads=[rpq, R("e192")], writes=[R("rq1")])
                    sc.op("dve", lambda e: e.reciprocal(out=rq[1][:, 0:T], in_=rq[1][:, 0:T]), reads=[R("rq1")], writes=[R("rq1")])
                    sc.op("dve", lambda e, pn=pn, hd=hd: e.scalar_tensor_tensor(out=QNs[:, hd, 0:T], in0=pn[:, 0:T], scalar=self.gq_s[:, 0:1], in1=rq[1][:, 0:T],
                                                                              op0=ALU.mult, op1=ALU.mult), reads=[rpn, R("gq"), R("rq1")], writes=[R("QNs")])
                    sc.op("dve", lambda e, pr=pr: e.scalar_tensor_tensor(out=ra[1][:, 0:T], in0=pr[0:64, 0:T], scalar=self.gq_s[0:64, 1:2], in1=rc[:, 0:T],
                                                                       op0=ALU.mult, op1=ALU.mult), reads=[rpr, R("gq"), R("rc")], writes=[R("ra1")])
                    sc.op("dve", lambda e, pw=pw: e.scalar_tensor_tensor(out=rb[1][:, 0:T], in0=pw[0:64, 0:T], scalar=self.gq_s[0:64, 2:3], in1=rs_[:, 0:T],
                                                                       op0=ALU.mult, op1=ALU.mult), reads=[rpw, R("gq"), R("rs")], writes=[R("rb1")])
                    sc.op("pool", lambda e: e.tensor_tensor(out=ra[1][:, 0:T], in0=ra[1][:, 0:T], in1=rb[1][:, 0:T], op=ALU.add),
                          reads=[R("ra1"), R("rb1")], writes=[R("ra1")])
                    sc.op("pool", lambda e, hd=hd: e.tensor_tensor(out=QRs[:, hd, 0:T], in0=ra[1][:, 0:T], in1=rq[1][0:64, 0:T], op=ALU.mult),
                          reads=[R("ra1"), R("rq1")], writes=[R("QRs")])
                sc.op("act", lambda e: e.activation(out=RKs[:, 0:nb, :], in_=ps_rk[:, 0:nb * 16].rearrange("p (b h two) -> p b h two", h=8, two=2)[:, :, :, 0], func=AF.Sqrt,
                                                    bias=e192[:], scale=1.0 / 192.0), reads=[r_rk, R("e192")], writes=[R("RKs")])
                sc.op("dve", lambda e: e.reciprocal(out=RKs[:, 0:nb, :], in_=RKs[:, 0:nb, :]), reads=[R("RKs")], writes=[R("RKs")])
                sc.op("dve", lambda e: e.tensor_scalar(out=RKs[:, 0:nb, :], in0=RKs[:, 0:nb, :], scalar1=SM_SCALE, scalar2=None, op0=ALU.mult),
                      reads=[R("RKs")], writes=[R("RKs")])
                sc.dma("rks", lambda e: e.dma_start(out=self.RK[t0:t0 + T, :].rearrange("(b p) h -> p b h", p=128), in_=RKs[:, 0:nb, :]), reads=[R("RKs")])
                sc.dma("qns", lambda e: e.dma_start(out=self.QN.rearrange("(h p) s -> p h s", p=128)[:, :, t0:t0 + T], in_=QNs[:, :, 0:T]), reads=[R("QNs")])
                sc.dma("qrs", lambda e: e.dma_start(out=self.QR.rearrange("(h p) s -> p h s", p=64)[:, :, t0:t0 + T], in_=QRs[:, :, 0:T]), reads=[R("QRs")])
                sc.dma("kns", lambda e: e.dma_start(out=self.KN.rearrange("(h p) s -> p h s", p=128)[:, :, t0:t0 + T], in_=KNs[:, :, 0:T]), reads=[R("KNs")])
                for b in range(nb):
                    for hh in range(2):
                        pb, rpb = bank()
                        for c in range(2):
                            sc.op("pe", lambda e, b=b, c=c, hh=hh, pb=pb: e.matmul(pb[:, 0:512], kvn[:, c, b * 128:(b + 1) * 128],
                                                                                 Wukv3[:, c, 1024 + hh * 512:1024 + (hh + 1) * 512], start=(c == 0), stop=(c == 1)),
                                  reads=[R("kvn"), R("Wukv")], writes=[rpb])
                        sc.op("act", lambda e, b=b, hh=hh, pb=pb: e.copy(out=Vs[:, b, hh * 512:(hh + 1) * 512], in_=pb[:, 0:512]), reads=[rpb], writes=[R("Vs")])
                sc.dma("vs", lambda e: e.dma_start(out=self.V[t0:t0 + T, :].rearrange("(b p) n -> p b n", p=128), in_=Vs[:, 0:nb, :]), reads=[R("Vs")])
                for g in range(4):
                    for xc in range(4):
                        pb, rpb = bank()
                        for c in range(2):
                            sc.op("pe", lambda e, g=g, xc=xc, c=c, pb=pb: e.matmul(pb[:, 0:T], FC3[:, c, xc * 128:(xc + 1) * 128], fT[:, 2 * g + c, 0:T],
                                                                                 start=(c == 0), stop=(c == 1)), reads=[R("FC"), R("fT")], writes=[rpb])
                        eng = "act" if (xc % 2 == 0) else "dve"
                        if eng == "act":
                            sc.op("act", lambda e, g=g, xc=xc, pb=pb: e.copy(out=GTs[:, g * 4 + xc, 0:T], in_=pb[:, 0:T]), reads=[rpb], writes=[R("GTs")])
                        else:
                            sc.op("dve", lambda e, g=g, xc=xc, pb=pb: e.tensor_copy(out=GTs[:, g * 4 + xc, 0:T], in_=pb[:, 0:T]), reads=[rpb], writes=[R("GTs")])
                sc.dma("gts", lambda e: e.dma_start(out=self.GT.rearrange("(c p) s -> p c s", p=128)[:, :, t0:t0 + T], in_=GTs[:, :, 0:T]), reads=[R("GTs")])
            for i, (t0, T, col) in enumerate(tl):
                tile_body(i, t0, T, col)
            sc.barrier()
            sc.flush()

    def phase_attn(self, l, last):
        sc, R = self.sc, self.R
        S, ST = self.S, self.ST
        NBL, NBT = S // 128, ST // 128
        with ExitStack() as st:
            ps = self.psum_banks(st, 8)
            KNhs = [self.sb(st, "KNh%d" % i, [128, ST], BF16) for i in range(2)]
            KRt = self.sb(st, "KRt", [128, ST], BF16)
            Vhs = [self.sb(st, "Vh%d" % i, [128, NBT, 128], BF16) for i in range(2)]
            RKa = self.sb(st, "RKa", [128, NBT, 8], F32)
            qn = [self.sb(st, "aqn%d" % i, [128, 512], BF16) for i in range(2)]
            qr = [self.sb(st, "aqr%d" % i, [128, 512], BF16) for i in range(2)]
            pT = [self.sb(st, "pT%d" % i, [128, 512], BF16) for i in range(4)]
            rden = [self.sb(st, "rden%d" % i, [128, 512], F32) for i in range(2)]
            ot = [self.sb(st, "ot%d" % i, [128, 512], BF16) for i in range(2)]
            self.dbo = self.sb(st, "dbo", [128, 512], F32)
            sc.op("dve", lambda e: e.memset(KRt[:], 0.0), writes=[R("KRt")])
            for i in range(2):
                sc.op("dve", lambda e, i=i: e.memset(qr[i][:], 0.0), writes=[R("aqr%d" % i)])
            CH = 4096
            for c0 in range(0, ST, CH):
                cl = min(CH, ST - c0)
                sc.dma("krl", lambda e, c0=c0, cl=cl: e.dma_start(out=KRt[0:64, c0:c0 + cl], in_=self.KR[:, c0:c0 + cl]), writes=[R("KRt")])
            for b0 in range(0, NBT, 32):
                bl = min(32, NBT - b0)
                sc.dma("rkl", lambda e, b0=b0, bl=bl: e.dma_start(
                    out=RKa[:, b0:b0 + bl, :], in_=self.RK[b0 * 128:(b0 + bl) * 128, :].rearrange("(b p) h -> p b h", p=128)), writes=[R("RKa")])
            qi = [0]

            def qtile(hd, t0, T, kbs):
                i = qi[0]
                qi[0] += 1
                s2 = i % 2
                KNh, Vh = KNhs[hd % 2], Vhs[hd % 2]
                rKN, rV = R("KNh%d" % (hd % 2)), R("Vh%d" % (hd % 2))
                rqn, rqr = R("aqn%d" % s2), R("aqr%d" % s2)
                sc.dma("aqn%d" % s2, lambda e: e.dma_start(out=qn[s2][:, 0:T], in_=self.QN[hd * 128:(hd + 1) * 128, t0:t0 + T]), writes=[rqn])
                sc.dma("aqr%d" % s2, lambda e: e.dma_start(out=qr[s2][0:64, 0:T], in_=self.QR[hd * 64:(hd + 1) * 64, t0:t0 + T]), writes=[rqr])
                po, rpo = ps[4 + s2], R("psb%d" % (4 + s2))
                pd, rpd = ps[6 + s2], R("psb%d" % (6 + s2))
                n = len(kbs)

                def qk(idx):
                    kb = kbs[idx]
                    b = idx % 4
                    sc.op("pe", lambda e: e.matmul(ps[b][:, 0:T], KNh[:, kb * 128:(kb + 1) * 128], qn[s2][:, 0:T], start=True, stop=False),
                          reads=[rKN, rqn], writes=[R("psb%d" % b)])
                    sc.op("pe", lambda e: e.matmul(ps[b][:, 0:T], KRt[:, kb * 128:(kb + 1) * 128], qr[s2][:, 0:T], start=False, stop=True),
                          reads=[R("KRt"), rqr], writes=[R("psb%d" % b)])

                def rest(idx):
                    kb = kbs[idx]
                    b = idx % 4
                    sc.op("act", lambda e: e.activation(out=pT[b][:, 0:T], in_=ps[b][:, 0:T], func=AF.Exp, bias=self.negB[:], scale=RKa[:, kb, hd:hd + 1]),
                          reads=[R("psb%d" % b), R("negB"), R("RKa")], writes=[R("pT%d" % b)])
                    if self.debug and i == 0:
                        sc.dma("dbgp", lambda e: e.dma_start(out=self.DBG_P[kb * 128:(kb + 1) * 128, 0:T], in_=pT[b][:, 0:T]), reads=[R("pT%d" % b)])
                    sc.op("pe", lambda e: e.matmul(po[:, 0:T], Vh[:, kb, :], pT[b][:, 0:T], start=(idx == 0), stop=(idx == n - 1)),
                          reads=[rV, R("pT%d" % b)], writes=[rpo])
                    sc.op("pe", lambda e: e.matmul(pd[:, 0:T], self.ones_b[:], pT[b][:, 0:T], start=(idx == 0), stop=(idx == n - 1)),
                          reads=[R("ones_b"), R("pT%d" % b)], writes=[rpd])
                qk(0)
                if n > 1:
                    qk(1)
                for idx in range(n):
                    if idx + 2 < n:
                        qk(idx + 2)
                    rest(idx)
                sc.op("dve", lambda e: e.reciprocal(out=rden[s2][:, 0:T], in_=pd[:, 0:T]), reads=[rpd], writes=[R("rden%d" % s2)])
                if self.debug and i == 0:
                    dbo = self.dbo
                    sc.op("dve", lambda e: e.tensor_copy(out=dbo[:, 0:T], in_=po[:, 0:T]), reads=[rpo], writes=[R("dbo")])
                    sc.dma("dbgo", lambda e: e.dma_start(out=self.DBG_O[:, 0:T], in_=dbo[:, 0:T]), reads=[R("dbo")])
                    sc.dma("dbgd", lambda e: e.dma_start(out=self.DBG_D[:, 0:T], in_=rden[s2][:, 0:T]), reads=[R("rden%d" % s2)])
                sc.op("dve", lambda e: e.tensor_tensor(out=ot[s2][:, 0:T], in0=po[:, 0:T], in1=rden[s2][:, 0:T], op=ALU.mult),
                      reads=[rpo, R("rden%d" % s2)], writes=[R("ot%d" % s2)])
                sc.dma("ots%d" % s2, lambda e: e.dma_start(out=self.AT[hd * 128:(hd + 1) * 128, t0:t0 + T], in_=ot[s2][:, 0:T]), reads=[R("ot%d" % s2)])

            vv = self.V.rearrange("(b p) (h d) -> p b h d", p=128, d=128)

            def load_head(hd):
                s_ = hd % 2
                for c0 in range(0, ST, CH):
                    cl = min(CH, ST - c0)
                    sc.dma("knl%d" % s_, lambda e, c0=c0, cl=cl: e.dma_start(out=KNhs[s_][:, c0:c0 + cl], in_=self.KN[hd * 128:(hd + 1) * 128, c0:c0 + cl]),
                           writes=[R("KNh%d" % s_)])
                for b0 in range(0, NBT, 16):
                    bl = min(16, NBT - b0)
                    sc.dma("vl%d" % s_, lambda e, b0=b0, bl=bl: e.dma_start(out=Vhs[s_][:, b0:b0 + bl, :], in_=vv[:, b0:b0 + bl, hd, :]),
                           writes=[R("Vh%d" % s_)])
            load_head(0)
            for hd in range(NH):
                if hd + 1 < NH:
                    load_head(hd + 1)
                for t0 in range(0, S, 512):
                    qtile(hd, t0, 512, list(range(NBT)))
                if not last:
                    qtile(hd, S, CTX, list(range(NBL, NBT)))
            sc.barrier()
            sc.flush()

    def phase_fourier(self, l, last):
        sc, R = self.sc, self.R
        S = self.S
        cases = [(S, self.L1, self.t1a, self.t1b, self.er, self.ei, 0)]
        if not last:
            cases.append((CTX, 2, self.t1ac, self.t1bc, self.erc, self.eic, S))
        for (Lc, L1, t1a, t1b, er, ei, tok0) in cases:
            with ExitStack() as st:
                ps = self.psum_banks(st, 8)
                stg = [self.sb(st, "stg%d" % i, [128, 2048], F32) for i in range(2)]
                Er = self.sb(st, "Er", [128, Lc], BF16)
                Ei = self.sb(st, "Ei", [128, Lc], BF16)
                Ta = self.sb(st, "Ta", [128, 2 * L1], BF16)
                Tb = self.sb(st, "Tb", [128, 2 * L1], BF16)
                self.load_cast(stg, Er, R("Er"), er, Lc, "er")
                self.load_cast(stg, Ei, R("Ei"), ei, Lc, "ei")
                self.load_cast([s_[0:L1, :] for s_ in stg], Ta[0:L1, :], R("Ta"), t1a, 2 * L1, "ta")
                self.load_cast([s_[0:L1, :] for s_ in stg], Tb[0:L1, :], R("Tb"), t1b, 2 * L1, "tb")
                Dt = [self.sb(st, "Dt%d" % i, [128, 128, 128], BF16) for i in range(1)]
                YT = self.sb(st, "YT", [128, 64, 2 * L1], BF16)
                MTb = self.sb(st, "MTb", [64, Lc], BF16)
                Er3 = Er[:].rearrange("p (k2 k1) -> p k1 k2", k1=L1)
                Ei3 = Ei[:].rearrange("p (k2 k1) -> p k1 k2", k1=L1)
                MT3 = MTb[:].rearrange("p (k2 k1) -> p k1 k2", k1=L1)
                nb = [0]

                def bank():
                    b = nb[0] % 8
                    nb[0] += 1
                    return ps[b], R("psb%d" % b)
                npb = 512 // (2 * L1) if 2 * L1 <= 512 else 1
                npb = min(npb, 64)
                kpb = min(4, L1)

                def block(g, cb):
                    Dd = Dt[0]
                    for part in range(2):
                        row0 = g * 512 + part * 256 + cb * 64
                        for r0 in range(0, 64, 16):
                            sc.dma("dtl", lambda e, part=part, r0=r0, row0=row0: e.dma_start(
                                out=Dd[0:L1, part * 64 + r0:part * 64 + r0 + 16, :],
                                in_=self.GT[row0 + r0:row0 + r0 + 16, tok0:tok0 + Lc].rearrange("r (a b) -> a r b", b=128)), writes=[R("Dt")])
                    for d0 in range(0, 64, npb):
                        pb, rpb = bank()
                        for dd in range(npb):
                            d = d0 + dd
                            o = dd * 2 * L1
                            sc.op("pe", lambda e, d=d, o=o, pb=pb: e.matmul(pb[:, o:o + 2 * L1], Dd[0:L1, d, :], Ta[0:L1, :], start=True, stop=False),
                                  reads=[R("Dt"), R("Ta")], writes=[rpb])
                            sc.op("pe", lambda e, d=d, o=o, pb=pb: e.matmul(pb[:, o:o + 2 * L1], Dd[0:L1, 64 + d, :], Tb[0:L1, :], start=False, stop=True),
                                  reads=[R("Dt"), R("Tb")], writes=[rpb])
                        eng = "act" if (d0 // npb) % 2 == 0 else "dve"
                        src = lambda pb=pb: pb[:, 0:npb * 2 * L1].rearrange("p (d x) -> p d x", x=2 * L1)
                        if eng == "act":
                            sc.op("act", lambda e, d0=d0, pb=pb: e.copy(out=YT[:, d0:d0 + npb, :], in_=pb[:, 0:npb * 2 * L1].rearrange("p (d x) -> p d x", x=2 * L1)),
                                  reads=[rpb], writes=[R("YT")])
                        else:
                            sc.op("dve", lambda e, d0=d0, pb=pb: e.tensor_copy(out=YT[:, d0:d0 + npb, :], in_=pb[:, 0:npb * 2 * L1].rearrange("p (d x) -> p d x", x=2 * L1)),
                                  reads=[rpb], writes=[R("YT")])
                    for k0 in range(0, L1, kpb):
                        pb, rpb = bank()
                        for kk in range(kpb):
                            k1 = k0 + kk
                            sc.op("pe", lambda e, k1=k1, kk=kk, pb=pb: e.matmul(pb[0:64, kk * 128:(kk + 1) * 128], YT[:, :, k1], Er3[:, k1, :], start=True, stop=False),
                                  reads=[R("YT"), R("Er")], writes=[rpb])
                            sc.op("pe", lambda e, k1=k1, kk=kk, pb=pb: e.matmul(pb[0:64, kk * 128:(kk + 1) * 128], YT[:, :, L1 + k1], Ei3[:, k1, :], start=False, stop=True),
                                  reads=[R("YT"), R("Ei")], writes=[rpb])
                        eng = "act" if (k0 // kpb) % 2 == 0 else "dve"
                        if eng == "act":
                            sc.op("act", lambda e, k0=k0, pb=pb: e.copy(out=MT3[:, k0:k0 + kpb, :], in_=pb[0:64, 0:kpb * 128].rearrange("p (q k) -> p q k", k=128)),
                                  reads=[rpb], writes=[R("MTb")])
                        else:
                            sc.op("dve", lambda e, k0=k0, pb=pb: e.tensor_copy(out=MT3[:, k0:k0 + kpb, :], in_=pb[0:64, 0:kpb * 128].rearrange("p (q k) -> p q k", k=128)),
                                  reads=[rpb], writes=[R("MTb")])
                    for c0 in range(0, Lc, 4096):
                        cl = min(4096, Lc - c0)
                        sc.dma("mts", lambda e, c0=c0, cl=cl: e.dma_start(out=self.MT[g * 256 + cb * 64:g * 256 + (cb + 1) * 64, tok0 + c0:tok0 + c0 + cl],
                                                                        in_=MTb[:, c0:c0 + cl]), reads=[R("MTb")])
                for g in range(4):
                    for cb in range(4):
                        block(g, cb)
                sc.barrier()
                sc.flush()

    def phase_mixD(self, l, last):
        sc, R = self.sc, self.R
        tl = self.tiles(with_ctx=not last)
        xv = self.xr.rearrange("(c p) s -> p c s", p=128)
        with ExitStack() as st:
            ps = self.psum_banks(st, 8)
            stg = [self.sb(st, "stg%d" % i, [128, 2048], F32) for i in range(2)]
            Wo = self.sb(st, "Wo", [128, KC * 2048], BF16)
            Wf = self.sb(st, "Wf", [128, 2048], BF16)
            self.load_cast(stg, Wo, R("Wo"), self.wo[l * 128:(l + 1) * 128, :], KC * 2048, "wo")
            self.load_cast(stg, Wf, R("Wf"), self.wf[l * 128:(l + 1) * 128, :], 2048, "wf")
            Wo3 = Wo[:].rearrange("p (c n) -> p c n", c=KC)
            Wf4 = Wf[:].rearrange("p (g c e) -> p g c e", g=4, c=2)
            X = [self.sb(st, "X%d" % i, [128, KC, 512], F32) for i in range(2)]
            A_ = [self.sb(st, "A_%d" % i, [128, KC, 512], BF16) for i in range(1)]
            M_ = [self.sb(st, "M_%d" % i, [128, 8, 512], BF16) for i in range(1)]
            atv = self.AT.rearrange("(c p) s -> p c s", p=128)
            mtv = self.MT.rearrange("(c p) s -> p c s", p=128)
            nb = [0]

            def bank():
                b = nb[0] % 8
                nb[0] += 1
                return ps[b], R("psb%d" % b)

            def tile_body(i, t0, T, col):
                s2 = i % 2
                Xi, Ai, Mi = X[s2], A_[0], M_[0]
                rX, rA, rM = R("X%d" % s2), R("A_0"), R("M_0")
                sc.dma("xl%d" % s2, lambda e: e.dma_start(out=Xi[:, :, 0:T], in_=xv[:, :, t0:t0 + T]), writes=[rX])
                sc.dma("al%d" % s2, lambda e: e.dma_start(out=Ai[:, 0:8, 0:T], in_=atv[:, :, t0:t0 + T]), writes=[rA])
                sc.dma("ml%d" % s2, lambda e: e.dma_start(out=Mi[:, :, 0:T], in_=mtv[:, :, t0:t0 + T]), writes=[rM])
                for g in range(4):
                    for ec in range(2):
                        pb, rpb = bank()
                        for c in range(2):
                            sc.op("pe", lambda e, g=g, ec=ec, c=c, pb=pb: e.matmul(pb[:, 0:T], Wf4[:, g, c, ec * 128:(ec + 1) * 128], Mi[:, 2 * g + c, 0:T],
                                                                                 start=(c == 0), stop=(c == 1)), reads=[R("Wf"), rM], writes=[rpb])
                        sc.op("act", lambda e, g=g, ec=ec, pb=pb: e.copy(out=Ai[:, 8 + 2 * g + ec, 0:T], in_=pb[:, 0:T]), reads=[rpb], writes=[rA])
                for j in range(KC):
                    pb, rpb = bank()
                    for c in range(KC):
                        sc.op("pe", lambda e, j=j, c=c, pb=pb: e.matmul(pb[:, 0:T], Wo3[:, c, j * 128:(j + 1) * 128], Ai[:, c, 0:T],
                                                                      start=(c == 0), stop=(c == KC - 1)), reads=[R("Wo"), rA], writes=[rpb])
                    sc.op("dve", lambda e, j=j, pb=pb: e.scalar_tensor_tensor(
                        out=Xi[:, j, 0:T], in0=pb[:, 0:T], scalar=self.Gat[:, 1, j, col:col + 1], in1=Xi[:, j, 0:T],
                        op0=ALU.mult, op1=ALU.add), reads=[rpb, R("Gat"), rX], writes=[rX])
                sc.dma("xs%d" % s2, lambda e: e.dma_start(out=xv[:, :, t0:t0 + T], in_=Xi[:, :, 0:T]), reads=[rX])
            for i, (t0, T, col) in enumerate(tl):
                tile_body(i, t0, T, col)
            sc.barrier()
            sc.flush()


def _tables(S):
    ST = S + CTX
    f32 = np.float32
    t = np.arange(S)
    row = (t // 64).astype(f32)
    colp = (t % 64).astype(f32)
    inv = (np.float32(10000.0) ** (-np.arange(16, dtype=f32) / np.float32(16))).astype(f32)
    ang = [row[None, :] * inv[:, None], colp[None, :] * inv[:, None]]
    rc = np.ones((64, ST), f32)
    rs = np.zeros((64, ST), f32)
    for p in range(64):
        half, pos, i = p // 32, (p % 32) // 16, p % 16
        rc[p, :S] = np.cos(ang[half][i]).astype(f32)
        sn = np.sin(ang[half][i]).astype(f32)
        rs[p, :S] = -sn if pos == 0 else sn
    c = np.arange(256)
    a = 2 * np.pi * ((c[:, None] * c[None, :]) % 256) / 256
    fc = np.concatenate([np.cos(a), -np.sin(a)], axis=1).astype(f32)
    fct = np.ascontiguousarray(fc.reshape(2, 128, 512).transpose(1, 0, 2).reshape(128, 1024))

    def dft(L):
        L1 = L // 128
        i1 = np.arange(L1)
        a1 = 2 * np.pi * ((i1[:, None] * i1[None, :]) % L1) / L1
        cr, ci = np.cos(a1), -np.sin(a1)
        t1a = np.concatenate([cr, ci], axis=1).astype(f32)
        t1b = np.concatenate([-ci, cr], axis=1).astype(f32)
        l2 = np.arange(128, dtype=np.int64)
        k = np.arange(L, dtype=np.int64)
        a2 = 2 * np.pi * ((l2[:, None] * k[None, :]) % L) / L
        nrm = 1.0 / np.sqrt(L * 256.0)
        er = (np.cos(a2) * nrm).astype(f32)
        ei = (np.sin(a2) * nrm).astype(f32)
        return t1a, t1b, er, ei
    t1a, t1b, er, ei = dft(S)
    t1ac, t1bc, erc, eic = dft(CTX)
    return dict(ropec=rc, ropes=rs, fct=fct, t1a=t1a, t1b=t1b, er=er, ei=ei, t1ac=t1ac, t1bc=t1bc, erc=erc, eic=eic)


def _prep_shared(inp, L):
    f = np.ascontiguousarray
    out = {}
    out["w_ada"] = inp["w_ada"][:L].reshape(L * D, 18432)
    out["b_adaT"] = f(inp["b_ada"][:L].reshape(L, 144, 128).transpose(0, 2, 1)).reshape(L * 128, 144)
    out["norm_gT"] = f(inp["norm_g"][:L].reshape(L, 48, 128).transpose(0, 2, 1)).reshape(L * 128, 48)
    wgu = np.empty((L, 2, HC, 128, 2, KC, 128), np.float32)
    wd = np.empty((L, 2, KC, 128, HC, 128), np.float32)
    ffn = ((inp["ffn1_w_gate"], inp["ffn1_w_up"], inp["ffn1_w_down"]),
           (inp["ffn2_w_gate"], inp["ffn2_w_up"], inp["ffn2_w_down"]))
    for s in range(2):
        for a in range(2):
            w = ffn[s][a][:L].reshape(L, KC, 128, HC, 128)
            wgu[:, s, :, :, a] = w.transpose(0, 3, 2, 1, 4)
        w = ffn[s][2][:L].reshape(L, HC, 128, KC, 128)
        wd[:, s] = w.transpose(0, 3, 2, 1, 4)
    out["wgu"] = wgu.reshape(L * 2 * HC * 128, 2 * KC * 128)
    out["wd"] = wd.reshape(L * 2 * KC * 128, HC * 128)
    perm = np.array([i + 16 if (i % 32) < 16 else i - 16 for i in range(64)])
    cols = np.concatenate([np.arange(0, 832), 768 + perm, np.arange(832, 1856)])
    w = inp["w_in"][:L][:, :, cols].reshape(L, KC, 128, 15, 128)
    out["win"] = f(w.transpose(0, 3, 2, 1, 4)).reshape(L * 15 * 128, KC * 128)
    hq = np.arange(NH)[:, None] * 192
    cq = np.concatenate([(hq + np.arange(128)[None]).ravel(), (hq + 128 + np.arange(64)[None]).ravel(), (hq + 128 + perm[None]).ravel()])
    w = inp["w_uq"][:L][:, :, cq].reshape(L, 4, 128, 2048)
    out["wuq"] = f(w.transpose(0, 2, 1, 3)).reshape(L * 128, 4 * 2048)
    hk = np.arange(NH)[:, None] * 256
    ck = np.concatenate([(hk + np.arange(128)[None]).ravel(), (hk + 128 + np.arange(128)[None]).ravel()])
    w = inp["w_ukv"][:L][:, :, ck].reshape(L, 2, 128, 2048)
    out["wukv"] = f(w.transpose(0, 2, 1, 3)).reshape(L * 128, 2 * 2048)
    w = inp["w_fourier"][:L].reshape(L, 4, 2, 128, 256)
    out["wf"] = f(w.transpose(0, 3, 1, 2, 4)).reshape(L * 128, 2048)
    w = inp["w_o"][:L].reshape(L, KC, 128, 2048)
    out["wo"] = f(w.transpose(0, 2, 1, 3)).reshape(L * 128, KC * 2048)
    out["gql"] = f(inp["q_lat_g"][:L].reshape(L, 4, 128).transpose(0, 2, 1)).reshape(L * 128, 4)
    out["gkvl"] = f(inp["kv_lat_g"][:L].reshape(L, 2, 128).transpose(0, 2, 1)).reshape(L * 128, 2)
    for nm, key in (("gq", "q_norm_g"), ("gk", "k_norm_g")):
        g = inp[key][:L]
        t = np.ones((L, 128, 3), np.float32)
        t[:, :, 0] = g[:, 0:128]
        t[:, 0:64, 1] = g[:, 128:192]
        t[:, 0:64, 2] = g[:, 128 + perm]
        out[nm] = t.reshape(L * 128, 3)
        out[nm + "b"] = f(np.broadcast_to(g[:, None, :], (L, 128, 192))).reshape(L * 128, 192)
    return {k: f(v, dtype=np.float32) if v.dtype != np.float32 else f(v) for k, v in out.items()}


_CACHE = {}


def kernel(**inputs):
    S, L, B = CFG["S"], CFG["DEPTH"], CFG["B"]
    inp = {k: np.asarray(v) for k, v in inputs.items()}
    key = (S, L)
    if key not in _CACHE:
        _CACHE[key] = Builder(S, L).build()
    nc = _CACHE[key]
    shared = _prep_shared(inp, L)
    shared.update(_tables(S))
    in_maps = []
    for b in range(B):
        m = dict(shared)
        m["xT"] = np.ascontiguousarray(inp["x"][b].T)
        m["cxT"] = np.ascontiguousarray(inp["ctx"][b].T)
        m["cT"] = np.ascontiguousarray(np.stack([inp["c"][b], inp["c_ctx"]], axis=1))
        in_maps.append(m)
    res = run_bass_kernel_spmd(nc, in_maps, core_ids=list(range(B)))
    out = np.stack([np.ascontiguousarray(res.results[b]["yT"].T) for b in range(B)], axis=0)
    return out.astype(np.float32)
```
